# Optimizing a Trainium2 kernel written in Bass

```python
import math
import jax
import jax.numpy as jnp
from jax import lax
import numpy as np

D_MODEL = 1024
BATCH = 32
SEQ = 256
DEPTH = 2
DEC_BATCH = 8
DEC_SEQ = 1024
PAST_LEN = 512

GRID_W = 64
HEAD_DIM = 64
ROPE_PAIRS = HEAD_DIM // 4
ROPE_BASE = 10000.0
BLOCK = 128
EPS = 1e-6
NEG_INF = -1e30
D_FF = 2816
N_MOD = 9
N_BRANCH = 4
BRANCH_W = 512

HY_W = BRANCH_W
HY_SHORT = 3
HY_BANDS = 16
HY_EMB = 1 + 2 * HY_BANDS
HY_FH = 64
HY_SIN_W = 1.0
HY_TARGET = 1e-2
HY_DECAY_SHORT_PCT = 0.3
HY_DECAY_LONG_PCT = 1.5

DIFF_HEADS = 4
DIFF_QK_W = DIFF_HEADS * 2 * HEAD_DIM
DIFF_V_W = DIFF_HEADS * 2 * HEAD_DIM

WIN_Q_HEADS = 8
WIN_KV_HEADS = 2
WIN_GROUP = WIN_Q_HEADS // WIN_KV_HEADS
WINDOW = 128
WIN_Q_W = WIN_Q_HEADS * HEAD_DIM
WIN_KV_W = WIN_KV_HEADS * HEAD_DIM

RET_HEADS = 4
RET_DK = 64
RET_DV = 128
RET_CHUNK = 128
RET_QK_W = RET_HEADS * RET_DK
RET_V_W = RET_HEADS * RET_DV

COL_SIZES = (3 * HY_W, DIFF_QK_W, DIFF_QK_W, DIFF_V_W, WIN_Q_W, WIN_KV_W, WIN_KV_W,
             RET_QK_W, RET_QK_W, RET_V_W, RET_V_W, N_BRANCH * D_MODEL)
IN_COLS = sum(COL_SIZES)

kernel_name = 'hybrid_diffusion_prefix_trunk_step'


def rms_norm(x, gain=None):
    xf = x.astype(jnp.float32)
    y = xf * lax.rsqrt(jnp.mean(xf * xf, axis=-1, keepdims=True) + EPS)
    if gain is not None:
        y = y * gain.astype(jnp.float32)
    return y.astype(x.dtype)


def ada_mod(cond, w, b):
    return jax.nn.silu(cond) @ w + b


def modulate(h, shift, scale):
    return h * (1.0 + scale) + shift


def swiglu(h, w_in, w_out):
    a, g = jnp.split(h @ w_in, 2, axis=-1)
    return (jax.nn.silu(a) * g) @ w_out


def axial_rope(x):
    n = x.shape[1]
    rows = n // GRID_W
    row = jnp.broadcast_to(jnp.arange(rows)[:, None], (rows, GRID_W)).reshape(n)
    col = jnp.broadcast_to(jnp.arange(GRID_W)[None, :], (rows, GRID_W)).reshape(n)
    inv_freq = ROPE_BASE ** (-jnp.arange(ROPE_PAIRS, dtype=jnp.float32) / ROPE_PAIRS)
    bshape = (n,) + (1,) * (x.ndim - 3) + (ROPE_PAIRS,)

    def rotate(xa, pos):
        ang = (pos.astype(jnp.float32)[:, None] * inv_freq[None, :]).reshape(bshape)
        cos, sin = jnp.cos(ang).astype(x.dtype), jnp.sin(ang).astype(x.dtype)
        x1, x2 = xa[..., :ROPE_PAIRS], xa[..., ROPE_PAIRS:]
        return jnp.concatenate([x1 * cos - x2 * sin, x1 * sin + x2 * cos], axis=-1)

    half = HEAD_DIM // 2
    return jnp.concatenate([rotate(x[..., :half], row), rotate(x[..., half:], col)], axis=-1)


def sweep_query_blocks(fn, q):
    B, L = q.shape[:2]
    nb = L // BLOCK
    qb = jnp.moveaxis(q.reshape((B, nb, BLOCK) + q.shape[2:]), 1, 0)
    out = lax.map(fn, qb)
    return jnp.moveaxis(out, 0, 1).reshape((B, L) + out.shape[3:])


def diff_block(qb, k, v, lam):
    s = jnp.einsum('bqhcd,bkhcd->bhcqk', qb, k).astype(jnp.float32) * (HEAD_DIM ** -0.5)
    p = jax.nn.softmax(s, axis=-1)
    a = p[:, :, 0] - lam * p[:, :, 1]
    return jnp.einsum('bhqk,bkhe->bqhe', a.astype(v.dtype), v)


def window_dense_block(qb, k, v, sink):
    B, Q = qb.shape[:2]
    s = jnp.einsum('bqhgd,bkhd->bhgqk', qb, k).astype(jnp.float32) * (HEAD_DIM ** -0.5)
    s_sink = jnp.broadcast_to(sink.astype(jnp.float32).reshape(WIN_KV_HEADS, WIN_GROUP)[None, :, :, None, None],
                              (B, WIN_KV_HEADS, WIN_GROUP, Q, 1))
    p = jax.nn.softmax(jnp.concatenate([s, s_sink], axis=-1), axis=-1)[..., :-1]
    return jnp.einsum('bhgqk,bkhd->bqhgd', p.astype(v.dtype), v)


def window_latent(q, k, v, k_ctx, v_ctx, sink):
    B, L = q.shape[:2]
    nb = L // BLOCK
    C = k_ctx.shape[1]
    qb = q.reshape(B, nb, BLOCK, WIN_KV_HEADS, WIN_GROUP, HEAD_DIM)

    def band(x):
        xp = jnp.pad(x, ((0, 0), (BLOCK, BLOCK), (0, 0), (0, 0))).reshape(B, nb + 2, BLOCK, WIN_KV_HEADS, HEAD_DIM)
        return jnp.concatenate([xp[:, :-2], xp[:, 1:-1], xp[:, 2:]], axis=2)

    kb, vb = band(k), band(v)
    blk = jnp.arange(nb)
    tq = blk[:, None] * BLOCK + jnp.arange(BLOCK)[None, :]
    tk = (blk[:, None] - 1) * BLOCK + jnp.arange(3 * BLOCK)[None, :]
    rel = tk[:, None, :] - tq[:, :, None]
    valid = (jnp.abs(rel) <= WINDOW) & (tk[:, None, :] >= 0) & (tk[:, None, :] < L)
    scale = HEAD_DIM ** -0.5
    s_band = jnp.einsum('bnqhgd,bnkhd->bnhgqk', qb, kb).astype(jnp.float32) * scale
    s_band = jnp.where(valid[None, :, None, None], s_band, NEG_INF)
    s_ctx = jnp.einsum('bnqhgd,bchd->bnhgqc', qb, k_ctx).astype(jnp.float32) * scale
    s_sink = jnp.broadcast_to(sink.astype(jnp.float32).reshape(WIN_KV_HEADS, WIN_GROUP)[None, None, :, :, None, None],
                              (B, nb, WIN_KV_HEADS, WIN_GROUP, BLOCK, 1))
    p = jax.nn.softmax(jnp.concatenate([s_band, s_ctx, s_sink], axis=-1), axis=-1)
    p_band = p[..., :3 * BLOCK].astype(v.dtype)
    p_ctx = p[..., 3 * BLOCK:3 * BLOCK + C].astype(v.dtype)
    o = (jnp.einsum('bnhgqk,bnkhd->bnqhgd', p_band, vb)
         + jnp.einsum('bnhgqc,bchd->bnqhgd', p_ctx, v_ctx))
    return o.reshape(B, L, WIN_KV_HEADS, WIN_GROUP, HEAD_DIM)


def retention_scan(q, k, v, log_gamma, s0):
    B, L, H, _ = q.shape
    nc = L // RET_CHUNK
    f32 = jnp.float32

    def chunks(x):
        return jnp.moveaxis(x.astype(f32).reshape((B, nc, RET_CHUNK) + x.shape[2:]), 1, 0)

    lg = log_gamma.astype(f32)
    idx = jnp.arange(RET_CHUNK, dtype=f32)
    rel = idx[:, None] - idx[None, :]
    intra = jnp.where(rel[None] >= 0, jnp.exp(jnp.maximum(rel, 0.0)[None] * lg[:, None, None]), 0.0)
    q_dec = jnp.exp((idx + 1.0)[:, None] * lg[None, :])[..., None]
    k_dec = jnp.exp((RET_CHUNK - 1.0 - idx)[:, None] * lg[None, :])[..., None]
    chunk_dec = jnp.exp(RET_CHUNK * lg)[None, :, None, None]

    def step(state, inp):
        qc, kc, vc = inp
        scores = jnp.einsum('bqhd,bkhd->bhqk', qc, kc) * intra
        out = (jnp.einsum('bhqk,bkhe->bqhe', scores, vc)
               + jnp.einsum('bqhd,bhde->bqhe', qc * q_dec, state))
        state = state * chunk_dec + jnp.einsum('bkhd,bkhe->bhde', kc * k_dec, vc)
        return state, out

    s_final, out = lax.scan(step, s0.astype(f32), (chunks(q), chunks(k), chunks(v)))
    return jnp.moveaxis(out, 0, 1).reshape(B, L, H, v.shape[-1]), s_final


def retention_bidirectional(q, k, v, lg_f, lg_b, s0_f, s0_b):
    o_f, s_f = retention_scan(q, k, v, lg_f, s0_f)
    o_b, s_b = retention_scan(jnp.flip(q, 1), jnp.flip(k, 1), jnp.flip(v, 1), lg_b, s0_b)
    return o_f + jnp.flip(o_b, 1), s_f, s_b


def short_conv(u, w, b):
    up = jnp.pad(u, ((0, 0), (1, 1), (0, 0)))
    return up[:, :-2] * w[0] + up[:, 1:-1] * w[1] + up[:, 2:] * w[2] + b


def hyena_filters(L, w1, b1, w2, b2, w3):
    t = jnp.arange(L, dtype=jnp.float32) / L
    f = jnp.arange(1, HY_BANDS + 1, dtype=jnp.float32)
    ang = 2.0 * math.pi * t[:, None] * f[None, :]
    feats = jnp.concatenate([t[:, None], jnp.sin(ang), jnp.cos(ang)], axis=-1)
    z = jnp.sin(HY_SIN_W * (feats @ w1 + b1))
    z = jnp.sin(HY_SIN_W * (z @ w2 + b2))
    z = (z @ w3).astype(jnp.float32)
    min_decay = math.log(HY_TARGET) / HY_DECAY_LONG_PCT
    max_decay = math.log(HY_TARGET) / HY_DECAY_SHORT_PCT
    deltas = jnp.linspace(min_decay, max_decay, HY_W, dtype=jnp.float32)
    window = jnp.exp(-t[:, None] * jnp.abs(deltas)[None, :])
    h_f = z[:, :HY_W] * window
    h_b = z[:, HY_W:] * window
    norm = jnp.sum(jnp.abs(h_f), axis=0, keepdims=True) + jnp.sum(jnp.abs(h_b), axis=0, keepdims=True)
    return h_f / norm, h_b / norm


def long_conv(u, h_f, h_b, skip):
    L = u.shape[1]
    k = jnp.concatenate([h_f, jnp.zeros_like(h_f[:1]), h_b[1:][::-1]], axis=0)
    k_f = jnp.fft.rfft(k, n=2 * L, axis=0)
    u_f = jnp.fft.rfft(u.astype(jnp.float32), n=2 * L, axis=1)
    y = jnp.fft.irfft(u_f * k_f[None], n=2 * L, axis=1)[:, :L]
    return (y + u.astype(jnp.float32) * skip.astype(jnp.float32)).astype(u.dtype)


def hyena_branch(cols, lp):
    L = cols.shape[1]
    u = short_conv(cols, lp['hy_conv_w'], lp['hy_conv_b'])
    v, x0, x1 = jnp.split(u, 3, axis=-1)
    h_f, h_b = hyena_filters(L, lp['hy_f_w1'], lp['hy_f_b1'], lp['hy_f_w2'], lp['hy_f_b2'], lp['hy_f_w3'])
    return x0 * long_conv(v * x1, h_f, h_b, lp['hy_skip'])


def mix(h, lp, layer_idx, cache):
    B, L, _ = h.shape
    f32 = jnp.float32
    split_points = np.cumsum(COL_SIZES)[:-1].tolist()
    (hy, dq, dk, dv, wq, wk, wv, rq, rk, rv, rg, gl) = jnp.split(h @ lp['w_in'], split_points, axis=-1)

    y_hy = hyena_branch(hy, lp)

    q_d = rms_norm(dq.reshape(B, L, DIFF_HEADS, 2, HEAD_DIM), lp['diff_q_norm'])
    k_d = rms_norm(dk.reshape(B, L, DIFF_HEADS, 2, HEAD_DIM), lp['diff_k_norm'])
    v_d = dv.reshape(B, L, DIFF_HEADS, 2 * HEAD_DIM)
    lam_init = 0.8 - 0.6 * math.exp(-0.3 * layer_idx)
    dl = lp['diff_lambda'].astype(f32)
    lam = jnp.exp(jnp.sum(dl[0] * dl[1])) - jnp.exp(jnp.sum(dl[2] * dl[3])) + lam_init

    q_w = rms_norm(wq.reshape(B, L, WIN_KV_HEADS, WIN_GROUP, HEAD_DIM), lp['win_q_norm'])
    k_w = rms_norm(wk.reshape(B, L, WIN_KV_HEADS, HEAD_DIM), lp['win_k_norm'])
    v_w = wv.reshape(B, L, WIN_KV_HEADS, HEAD_DIM)

    q_r = rq.reshape(B, L, RET_HEADS, RET_DK)
    k_r = rk.reshape(B, L, RET_HEADS, RET_DK) * (RET_DK ** -0.5)
    v_r = rv.reshape(B, L, RET_HEADS, RET_DV)
    lg_f = -jax.nn.softplus(-lp['ret_decay_f'].astype(f32))
    lg_b = -jax.nn.softplus(-lp['ret_decay_b'].astype(f32))

    if cache is None:
        s0 = jnp.zeros((B, RET_HEADS, RET_DK, RET_DV), f32)
        y_diff = sweep_query_blocks(lambda qb: diff_block(qb, k_d, v_d, lam), q_d)
        y_win = sweep_query_blocks(lambda qb: window_dense_block(qb, k_w, v_w, lp['win_sink']), q_w)
        y_ret, s_f, s_b = retention_bidirectional(q_r, k_r, v_r, lg_f, lg_b, s0, s0)
        state = (k_d, v_d, k_w, v_w, s_f, s_b)
    else:
        ck_d, cv_d, ck_w, cv_w, cs_f, cs_b = cache
        q_dr, k_dr = axial_rope(q_d), axial_rope(k_d)
        keys_d = jnp.concatenate([k_dr, ck_d.astype(k_dr.dtype)], axis=1)
        vals_d = jnp.concatenate([v_d, cv_d.astype(v_d.dtype)], axis=1)
        y_diff = sweep_query_blocks(lambda qb: diff_block(qb, keys_d, vals_d, lam), q_dr)
        y_win = window_latent(axial_rope(q_w), axial_rope(k_w), v_w,
                              ck_w.astype(k_w.dtype), cv_w.astype(v_w.dtype), lp['win_sink'])
        y_ret, _, _ = retention_bidirectional(q_r, k_r, v_r, lg_f, lg_b, cs_f, cs_b)
        state = None

    y_diff = (rms_norm(y_diff, lp['diff_subln']) * (1.0 - lam_init)).reshape(B, L, DIFF_V_W)
    y_win = y_win.reshape(B, L, WIN_Q_W)
    y_ret = rms_norm(y_ret).reshape(B, L, RET_V_W).astype(h.dtype) * jax.nn.silu(rg)

    gates = jax.nn.sigmoid(gl.reshape(B, L, N_BRANCH, D_MODEL))
    branches = (y_hy, y_diff, y_win, y_ret)
    merged = gates[:, :, 0] * (branches[0] @ lp['w_branch'][0])
    for i in range(1, N_BRANCH):
        merged = merged + gates[:, :, i] * (branches[i] @ lp['w_branch'][i])
    return merged @ lp['w_out'], state


def trunk_layer(x, mod, lp, layer_idx, cache):
    sa, ca, ga, sm, cm, gm, sb, cb, gb = jnp.split(mod[:, None, :], N_MOD, axis=-1)
    h = modulate(rms_norm(x, lp['norm_ffa']), sa, ca)
    x = x + 0.5 * ga * swiglu(h, lp['w_ffa_in'], lp['w_ffa_out'])
    h = modulate(rms_norm(x, lp['norm_mix']), sm, cm)
    mixed, state = mix(h, lp, layer_idx, cache)
    x = x + gm * mixed
    h = modulate(rms_norm(x, lp['norm_ffb']), sb, cb)
    x = x + 0.5 * gb * swiglu(h, lp['w_ffb_in'], lp['w_ffb_out'])
    return x, state


def setup_inputs(seed: int = 0) -> dict:
    key = jax.random.key(seed)
    ks = iter(jax.random.split(key, 48))

    def nrm(shape, scale):
        return jax.random.normal(next(ks), shape, jnp.float32) * scale

    def gain(shape):
        return 1.0 + nrm(shape, 0.02)

    ret_base = jnp.log(2.0 ** (5.0 + jnp.arange(RET_HEADS, dtype=jnp.float32)) - 1.0)
    inp = {}
    inp['x_prompt'] = nrm((BATCH, SEQ, D_MODEL), 1.0)
    inp['x_sample'] = nrm((DEC_BATCH, DEC_SEQ, D_MODEL), 1.0)
    inp['c'] = nrm((DEC_BATCH, D_MODEL), 1.0)
    inp['cache_diff_k'] = nrm((DEC_BATCH, DEPTH, PAST_LEN, DIFF_HEADS, 2, HEAD_DIM), 1.0)
    inp['cache_diff_v'] = nrm((DEC_BATCH, DEPTH, PAST_LEN, DIFF_HEADS, 2 * HEAD_DIM), 1.0)
    inp['cache_win_k'] = nrm((DEC_BATCH, DEPTH, PAST_LEN, WIN_KV_HEADS, HEAD_DIM), 1.0)
    inp['cache_win_v'] = nrm((DEC_BATCH, DEPTH, PAST_LEN, WIN_KV_HEADS, HEAD_DIM), 1.0)
    inp['state_ret_f'] = nrm((DEC_BATCH, DEPTH, RET_HEADS, RET_DK, RET_DV), 0.5)
    inp['state_ret_b'] = nrm((DEC_BATCH, DEPTH, RET_HEADS, RET_DK, RET_DV), 0.5)
    inp['c_ctx'] = nrm((D_MODEL,), 1.0)
    inp['norm_ffa'] = gain((DEPTH, D_MODEL))
    inp['norm_mix'] = gain((DEPTH, D_MODEL))
    inp['norm_ffb'] = gain((DEPTH, D_MODEL))
    inp['w_ada'] = nrm((DEPTH, D_MODEL, N_MOD * D_MODEL), 0.5 * D_MODEL ** -0.5)
    inp['b_ada'] = nrm((DEPTH, N_MOD * D_MODEL), 0.02)
    inp['w_ffa_in'] = nrm((DEPTH, D_MODEL, 2 * D_FF), D_MODEL ** -0.5)
    inp['w_ffa_out'] = nrm((DEPTH, D_FF, D_MODEL), D_FF ** -0.5)
    inp['w_ffb_in'] = nrm((DEPTH, D_MODEL, 2 * D_FF), D_MODEL ** -0.5)
    inp['w_ffb_out'] = nrm((DEPTH, D_FF, D_MODEL), D_FF ** -0.5)
    inp['w_in'] = nrm((DEPTH, D_MODEL, IN_COLS), D_MODEL ** -0.5)
    inp['hy_conv_w'] = nrm((DEPTH, HY_SHORT, 3 * HY_W), HY_SHORT ** -0.5)
    inp['hy_conv_b'] = nrm((DEPTH, 3 * HY_W), 0.02)
    inp['hy_f_w1'] = nrm((DEPTH, HY_EMB, HY_FH), HY_EMB ** -0.5)
    inp['hy_f_b1'] = nrm((DEPTH, HY_FH), 0.02)
    inp['hy_f_w2'] = nrm((DEPTH, HY_FH, HY_FH), HY_FH ** -0.5)
    inp['hy_f_b2'] = nrm((DEPTH, HY_FH), 0.02)
    inp['hy_f_w3'] = nrm((DEPTH, HY_FH, 2 * HY_W), HY_FH ** -0.5)
    inp['hy_skip'] = nrm((DEPTH, HY_W), 0.5)
    inp['diff_q_norm'] = gain((DEPTH, HEAD_DIM))
    inp['diff_k_norm'] = gain((DEPTH, HEAD_DIM))
    inp['diff_lambda'] = nrm((DEPTH, 4, HEAD_DIM), 0.1)
    inp['diff_subln'] = gain((DEPTH, 2 * HEAD_DIM))
    inp['win_q_norm'] = gain((DEPTH, HEAD_DIM))
    inp['win_k_norm'] = gain((DEPTH, HEAD_DIM))
    inp['win_sink'] = nrm((DEPTH, WIN_Q_HEADS), 0.5)
    inp['ret_decay_f'] = ret_base[None, :] + nrm((DEPTH, RET_HEADS), 0.1)
    inp['ret_decay_b'] = ret_base[None, :] + nrm((DEPTH, RET_HEADS), 0.1)
    inp['w_branch'] = nrm((DEPTH, N_BRANCH, BRANCH_W, D_MODEL), BRANCH_W ** -0.5)
    inp['w_out'] = nrm((DEPTH, D_MODEL, D_MODEL), D_MODEL ** -0.5)
    return inp


def reference(x_prompt, x_sample, c, cache_diff_k, cache_diff_v, cache_win_k, cache_win_v,
              state_ret_f, state_ret_b, c_ctx, norm_ffa, norm_mix, norm_ffb, w_ada, b_ada,
              w_ffa_in, w_ffa_out, w_ffb_in, w_ffb_out, w_in, hy_conv_w, hy_conv_b,
              hy_f_w1, hy_f_b1, hy_f_w2, hy_f_b2, hy_f_w3, hy_skip,
              diff_q_norm, diff_k_norm, diff_lambda, diff_subln,
              win_q_norm, win_k_norm, win_sink, ret_decay_f, ret_decay_b, w_branch, w_out):
    y_ctx = x_prompt
    y_lat = x_sample
    st_dk, st_dv, st_wk, st_wv, st_rf, st_rb = [], [], [], [], [], []
    for l in range(DEPTH):
        lp = {
            'norm_ffa': norm_ffa[l], 'norm_mix': norm_mix[l], 'norm_ffb': norm_ffb[l],
            'w_ffa_in': w_ffa_in[l], 'w_ffa_out': w_ffa_out[l],
            'w_ffb_in': w_ffb_in[l], 'w_ffb_out': w_ffb_out[l],
            'w_in': w_in[l], 'hy_conv_w': hy_conv_w[l], 'hy_conv_b': hy_conv_b[l],
            'hy_f_w1': hy_f_w1[l], 'hy_f_b1': hy_f_b1[l], 'hy_f_w2': hy_f_w2[l], 'hy_f_b2': hy_f_b2[l],
            'hy_f_w3': hy_f_w3[l], 'hy_skip': hy_skip[l],
            'diff_q_norm': diff_q_norm[l], 'diff_k_norm': diff_k_norm[l],
            'diff_lambda': diff_lambda[l], 'diff_subln': diff_subln[l],
            'win_q_norm': win_q_norm[l], 'win_k_norm': win_k_norm[l], 'win_sink': win_sink[l],
            'ret_decay_f': ret_decay_f[l], 'ret_decay_b': ret_decay_b[l],
            'w_branch': w_branch[l], 'w_out': w_out[l],
        }
        mod_ctx = ada_mod(c_ctx[None, :], w_ada[l], b_ada[l])
        y_ctx, ctx_state = trunk_layer(y_ctx, mod_ctx, lp, l, None)
        st_dk.append(ctx_state[0])
        st_dv.append(ctx_state[1])
        st_wk.append(ctx_state[2])
        st_wv.append(ctx_state[3])
        st_rf.append(ctx_state[4])
        st_rb.append(ctx_state[5])
        mod_lat = ada_mod(c, w_ada[l], b_ada[l])
        cache = (cache_diff_k[:, l], cache_diff_v[:, l], cache_win_k[:, l], cache_win_v[:, l],
                 state_ret_f[:, l], state_ret_b[:, l])
        y_lat, _ = trunk_layer(y_lat, mod_lat, lp, l, cache)
    new_diff_k = jnp.stack(st_dk, axis=1)
    new_diff_v = jnp.stack(st_dv, axis=1)
    new_win_k = jnp.stack(st_wk, axis=1)
    new_win_v = jnp.stack(st_wv, axis=1)
    new_ret_f = jnp.stack(st_rf, axis=1)
    new_ret_b = jnp.stack(st_rb, axis=1)
    return (y_ctx, y_lat, new_diff_k, new_diff_v, new_win_k, new_win_v, new_ret_f, new_ret_b)
```

```python
import math
from contextlib import ExitStack

import numpy as np
import concourse.bass as bass
import concourse.mybir as mybir
from concourse.bass_utils import run_bass_kernel_spmd

F32 = mybir.dt.float32
BF16 = mybir.dt.bfloat16
I32 = mybir.dt.int32
AF = mybir.ActivationFunctionType
ALU = mybir.AluOpType
AX = mybir.AxisListType

NCORES = 8
import os as _os
_NOSELFW = bool(_os.environ.get('NOSELFW'))
_BARRIERS = bool(_os.environ.get('BARRIERS'))
D = 1024
KD = 8
DFF = 2816
NJ = 22
T = 1024
NDMASEM = 12
EPS = 1e-6


def _isap(x):
    return hasattr(x, "tensor") and hasattr(x, "ap")


_TINFO = {}


def _region(ap):
    t = ap.tensor
    pat = ap.ap
    pstride = pat[0][0]
    off = ap.offset
    if pstride == 0:
        pstride = 1 << 40
    p_lo = off // pstride
    f_lo = off % pstride
    p_hi = p_lo + pat[0][1]
    ext = 1
    for st, cnt in pat[1:]:
        ext += abs(st) * (cnt - 1)
    ti = _TINFO.get(t.name)
    if ti is not None:
        return ("SB", p_lo, p_hi, ti[0] + f_lo * ti[1], ti[0] + (f_lo + ext) * ti[1])
    return (t.name, p_lo, p_hi, f_lo, f_lo + ext)


_PAGE = 2048


def _keys(name, fl, fh):
    if name != "SB":
        return (name,)
    return [("SB", pg) for pg in range(fl // _PAGE, (fh - 1) // _PAGE + 1)]


class Prog:
    def __init__(self, nc, stack):
        self.nc = nc
        self.eng = {"pe": nc.tensor, "act": nc.scalar, "dve": nc.vector, "pool": nc.gpsimd, "sp": nc.sync}
        self.sems = {}
        self.count = {}
        for e in self.eng:
            self.sems[e] = stack.enter_context(nc.semaphore("s_" + e))
            self.count[e] = 0
        self.dsem = {}
        self.dcount = {}
        self.dnext = {}
        for q in ("sp", "pool"):
            self.dsem[q] = [stack.enter_context(nc.semaphore("d_%s%d" % (q, i))) for i in range(NDMASEM)]
            self.dcount[q] = [0] * NDMASEM
            self.dnext[q] = 0
        self.seen = {e: {} for e in self.eng}
        self.snap = {}
        self.recs = {}
        self.nwaits = 0
        self.ninst = 0
        self.banks = []
        self.rot_i = 0
        self.nrot = 4

    def _deps(self, reads, writes, engine):
        need = {}
        for ap in reads:
            name, pl, ph, fl, fh = _region(ap)
            psum = name != "SB"
            for key in _keys(name, fl, fh):
                for r in self.recs.get(key, ()):
                    if psum:
                        if r[5] != engine and r[0] < ph and pl < r[1]:
                            if need.get(r[5], 0) < r[6]:
                                need[r[5]] = r[6]
                    if r[4] == "W" and r[0] < ph and pl < r[1] and r[2] < fh and fl < r[3]:
                        if need.get(r[5], 0) < r[6]:
                            need[r[5]] = r[6]
        for ap in writes:
            name, pl, ph, fl, fh = _region(ap)
            if name != "SB":
                fl, fh = 0, 1 << 30
            for key in _keys(name, fl, fh):
                for r in self.recs.get(key, ()):
                    if r[0] < ph and pl < r[1] and r[2] < fh and fl < r[3]:
                        if r[5] == engine and (engine == "pe" or _NOSELFW):
                            continue
                        if need.get(r[5], 0) < r[6]:
                            need[r[5]] = r[6]
        return need

    def _record(self, reads, writes, key, val):
        for ap in writes:
            name, pl, ph, fl, fh = _region(ap)
            rec = [pl, ph, fl, fh, "W", key, val]
            for k in _keys(name, fl, fh):
                lst = self.recs.setdefault(k, [])
                lst[:] = [r for r in lst if not (pl <= r[0] and r[1] <= ph and fl <= r[2] and r[3] <= fh)]
                lst.append(rec)
        for ap in reads:
            name, pl, ph, fl, fh = _region(ap)
            rec = [pl, ph, fl, fh, "R", key, val]
            for k in _keys(name, fl, fh):
                lst = self.recs.setdefault(k, [])
                lst[:] = [r for r in lst if not (r[4] == "R" and r[5] == key and pl <= r[0] and r[1] <= ph
                                                 and fl <= r[2] and r[3] <= fh)]
                lst.append(rec)

    def _semh(self, key):
        if isinstance(key, str):
            return self.sems[key]
        return self.dsem[key[0]][key[1]]

    def _wait(self, engine, need):
        seen = self.seen[engine]
        h = self.eng[engine]
        for k, v in need.items():
            if seen.get(k, 0) >= v:
                continue
            h.wait_ge(self._semh(k), v)
            self.nwaits += 1
            sn = self.snap.get((k, v))
            if sn:
                for kk, vv in sn.items():
                    if seen.get(kk, 0) < vv:
                        seen[kk] = vv
            seen[k] = v

    @staticmethod
    def _onchip(ap):
        return "DRam" not in ap.tensor.__class__.__name__

    def op(self, engine, fn, reads=(), writes=()):
        reads = [a for a in reads if _isap(a) and self._onchip(a)]
        writes = [a for a in writes if _isap(a) and self._onchip(a)]
        need = self._deps(reads, writes, engine)
        self._wait(engine, need)
        ins = fn(self.eng[engine])
        self.count[engine] += 1
        c = self.count[engine]
        ins.then_inc(self.sems[engine], 1)
        sn = dict(self.seen[engine])
        sn[engine] = c
        self.snap[(engine, c)] = sn
        self._record(reads, writes, engine, c)
        self.ninst += 1

    def dma(self, out, in_, queue="sp"):
        reads = [in_] if self._onchip(in_) else []
        writes = [out] if self._onchip(out) else []
        need = self._deps(reads, writes, None)
        i = self.dnext[queue]
        self.dnext[queue] = (i + 1) % NDMASEM
        key = (queue, i)
        prev = self.dcount[queue][i]
        if prev:
            need[key] = max(need.get(key, 0), prev)
        self._wait(queue, need)
        ins = self.eng[queue].dma_start(out=out, in_=in_)
        val = prev + 16
        ins.then_inc(self.dsem[queue][i], 16)
        self.dcount[queue][i] = val
        self.snap[(key, val)] = dict(self.seen[queue])
        self._record(reads, writes, key, val)
        self.ninst += 1

    def barrier(self):
        if not _BARRIERS:
            return
        need = {}
        for q in self.dsem:
            for i, v in enumerate(self.dcount[q]):
                if v:
                    need[(q, i)] = v
        for e in self.eng:
            if self.count[e]:
                need[e] = self.count[e]
        for e in self.eng:
            self._wait(e, {k: v for k, v in need.items() if k != e})
        self.recs = {}

    def finish(self):
        need = {}
        for q in self.dsem:
            for i, v in enumerate(self.dcount[q]):
                if v:
                    need[(q, i)] = v
        for e in self.eng:
            if e != "sp" and self.count[e]:
                need[e] = self.count[e]
        self._wait("sp", need)

    def rot(self):
        b = self.banks[self.rot_i % self.nrot]
        self.rot_i += 1
        return b

    def mm(self, out, lhsT, rhs, start=True, stop=True):
        self.op("pe", lambda e: e.matmul(out, lhsT=lhsT, rhs=rhs, start=start, stop=stop),
                reads=[lhsT, rhs], writes=[out])

    def tr(self, out, in_, ident):
        self.op("pe", lambda e: e.transpose(out, in_, ident), reads=[in_, ident], writes=[out])

    def act(self, out, in_, func, bias=None, scale=None):
        kw = {}
        if bias is not None:
            kw["bias"] = bias
        if scale is not None:
            kw["scale"] = scale
        self.op("act", lambda e: e.activation(out=out, in_=in_, func=func, **kw),
                reads=[in_, bias, scale], writes=[out])

    def copy(self, out, in_, eng="dve"):
        if eng == "act":
            self.op("act", lambda e: e.activation(out=out, in_=in_, func=AF.Copy), reads=[in_], writes=[out])
        else:
            self.op(eng, lambda e: e.tensor_copy(out=out, in_=in_), reads=[in_], writes=[out])

    def tt(self, out, a, b, op, eng="dve"):
        self.op(eng, lambda e: e.tensor_tensor(out=out, in0=a, in1=b, op=op), reads=[a, b], writes=[out])

    def ts(self, out, a, s1, op0, s2=None, op1=None, eng="dve"):
        if s2 is None:
            self.op(eng, lambda e: e.tensor_single_scalar(out=out, in_=a, scalar=s1, op=op0),
                    reads=[a, s1], writes=[out])
        else:
            self.op(eng, lambda e: e.tensor_scalar(out=out, in0=a, scalar1=s1, scalar2=s2, op0=op0, op1=op1),
                    reads=[a, s1, s2], writes=[out])

    def stt(self, out, a, s, b, op0, op1):
        self.op("dve", lambda e: e.scalar_tensor_tensor(out=out, in0=a, scalar=s, in1=b, op0=op0, op1=op1),
                reads=[a, s, b], writes=[out])

    def recip(self, out, in_):
        self.op("dve", lambda e: e.reciprocal(out=out, in_=in_), reads=[in_], writes=[out])

    def memset(self, ap, val, eng="dve"):
        self.op(eng, lambda e: e.memset(ap, val), writes=[ap])


class Ring:
    def __init__(self, items):
        self.items = items
        self.i = 0

    def __call__(self):
        x = self.items[self.i % len(self.items)]
        self.i += 1
        return x


PP_FIELDS = [("g0", 8), ("g1", 8), ("g2", 8), ("bada", 72), ("cw", 36), ("cb", 12), ("hb1", 1), ("hb2", 1),
             ("skip", 4), ("dqn", 1), ("dkn", 1), ("dsub", 1), ("wqn", 1), ("wkn", 1), ("dknrow", 64),
             ("wknrow", 64), ("dlam", 256), ("sink", 8), ("rdf", 4), ("rdb", 4)]
PP_OFF = {}
_o = 0
for _n, _w in PP_FIELDS:
    PP_OFF[_n] = (_o, _w)
    _o += _w
PP_W = _o

CC_FIELDS = [("ident", 128), ("blk64", 128), ("perm", 128), ("rel", 128), ("mle", 128), ("mge", 128),
             ("iota1", 128), ("iotar", 128), ("cola", 1), ("colb", 1)]
CC_OFF = {}
_o = 0
for _n, _w in CC_FIELDS:
    CC_OFF[_n] = (_o, _w)
    _o += _w
CC_W = _o


def _host_consts():
    p = np.arange(128)[:, None].astype(np.float64)
    i = np.arange(128)[None, :].astype(np.float64)
    cc = np.zeros((128, CC_W), np.float64)

    def put(name, arr):
        o, w = CC_OFF[name]
        cc[:, o:o + w] = arr

    put("ident", (p == i))
    put("blk64", ((p // 64) == (i // 64)) / 64.0)
    partner = np.where((np.arange(128) % 32) < 16, np.arange(128) + 16, np.arange(128) - 16)
    perm = np.zeros((128, 128))
    perm[partner, np.arange(128)] = 1.0
    put("perm", perm)
    put("rel", i - p)
    put("mle", (p <= i))
    put("mge", (p >= i))
    put("iota1", np.broadcast_to(i + 1.0, (128, 128)))
    put("iotar", np.broadcast_to(128.0 - i, (128, 128)))
    put("cola", 127.0 - p)
    put("colb", p)
    t = np.arange(1024)
    dd = np.arange(128) % 64
    half = dd // 32
    j = dd % 16
    inv_freq = 10000.0 ** (-j / 16.0)
    pos = np.where(half[:, None] == 0, (t // 64)[None, :], (t % 64)[None, :]).astype(np.float64)
    ang = pos * inv_freq[:, None]
    sgn = np.where((np.arange(128) % 32) < 16, -1.0, 1.0)[:, None]
    rope = np.concatenate([np.cos(ang), sgn * np.sin(ang)], axis=1)
    out = {"cc": cc.astype(np.float32), "rope": rope.astype(np.float32)}
    min_decay = math.log(1e-2) / 1.5
    max_decay = math.log(1e-2) / 0.3
    deltas = np.abs(np.linspace(min_decay, max_decay, 512))
    feats_all = np.zeros((128, 1280), np.float64)
    o = 0
    for L in (256, 1024):
        tt_ = np.arange(L) / L
        f = np.arange(1, 17)
        a = 2.0 * math.pi * tt_[:, None] * f[None, :]
        feats = np.concatenate([tt_[:, None], np.sin(a), np.cos(a)], axis=1)
        feats_all[:33, o:o + L] = feats.T
        o += L
        out["win%d" % L] = np.exp(-tt_[:, None] * deltas[None, :]).astype(np.float32)
        th = math.pi * (2.0 * np.arange(L)[None, :] + 1.0) * np.arange(L)[:, None] / (2.0 * L)
        fwd = np.concatenate([np.cos(th), np.sin(th)], axis=1)
        out["fwd%d" % L] = fwd.astype(np.float32)
        out["inv%d" % L] = (fwd.T / L).astype(np.float32)
    out["feats"] = feats_all.astype(np.float32)
    return out


def _host_params(inp):
    pp = np.zeros((2, 128, PP_W), np.float32)

    def put(l, name, arr):
        o, w = PP_OFF[name]
        pp[l][:, o:o + w] = np.asarray(arr, np.float32).reshape(128, w)

    for l in range(2):
        put(l, "g0", inp["norm_ffa"][l].reshape(8, 128).T)
        put(l, "g1", inp["norm_mix"][l].reshape(8, 128).T)
        put(l, "g2", inp["norm_ffb"][l].reshape(8, 128).T)
        put(l, "bada", inp["b_ada"][l].reshape(72, 128).T)
        put(l, "cw", inp["hy_conv_w"][l].reshape(3, 12, 128).transpose(2, 1, 0).reshape(128, 36))
        put(l, "cb", inp["hy_conv_b"][l].reshape(12, 128).T)
        hb = np.zeros((128, 1), np.float32)
        hb[:64, 0] = inp["hy_f_b1"][l]
        put(l, "hb1", hb)
        hb = np.zeros((128, 1), np.float32)
        hb[:64, 0] = inp["hy_f_b2"][l]
        put(l, "hb2", hb)
        put(l, "skip", inp["hy_skip"][l].reshape(4, 128).T)
        put(l, "dqn", np.tile(inp["diff_q_norm"][l], 2)[:, None])
        put(l, "dkn", np.tile(inp["diff_k_norm"][l], 2)[:, None])
        put(l, "dsub", inp["diff_subln"][l][:, None])
        put(l, "wqn", np.tile(inp["win_q_norm"][l], 2)[:, None])
        put(l, "wkn", np.tile(inp["win_k_norm"][l], 2)[:, None])
        put(l, "dknrow", np.broadcast_to(inp["diff_k_norm"][l][None, :], (128, 64)))
        put(l, "wknrow", np.broadcast_to(inp["win_k_norm"][l][None, :], (128, 64)))
        put(l, "dlam", np.broadcast_to(inp["diff_lambda"][l].reshape(1, 256), (128, 256)))
        put(l, "sink", np.broadcast_to(inp["win_sink"][l][None, :], (128, 8)))
        put(l, "rdf", np.broadcast_to(inp["ret_decay_f"][l][None, :], (128, 4)))
        put(l, "rdb", np.broadcast_to(inp["ret_decay_b"][l][None, :], (128, 4)))
    return pp


def build_program(stop_after=None, dbg=False, groups=(0, 1)):
    nc = bass.Bass("TRN2", target_bir_lowering=False)
    uid = [0]

    def din(name, shape):
        return nc.dram_tensor(name, list(shape), F32, kind="ExternalInput").ap()

    def dout(name, shape):
        return nc.dram_tensor(name, list(shape), F32, kind="ExternalOutput").ap()

    x_in = [din("x_ctx", [T, D]), din("x_lat", [T, D])]
    cond_d = din("cond", [128, 16])
    cdk = din("cdk", [2, 512, 512])
    cdv = din("cdv", [2, 512, 512])
    cwk = din("cwk", [2, 512, 128])
    cwv = din("cwv", [2, 512, 128])
    srf = din("srf", [2, 4, 64, 128])
    srb = din("srb", [2, 4, 64, 128])
    w_ada = din("w_ada", [2, D, 9 * D])
    w_ff_in = [din("w_ffa_in", [2, D, 2 * DFF]), din("w_ffb_in", [2, D, 2 * DFF])]
    w_ff_out = [din("w_ffa_out", [2, DFF, D]), din("w_ffb_out", [2, DFF, D])]
    w_in = din("w_in", [2, D, 9472])
    hy_w1 = din("hy_f_w1", [2, 33, 64])
    hy_w2 = din("hy_f_w2", [2, 64, 64])
    hy_w3 = din("hy_f_w3", [2, 64, 1024])
    w_branch = din("w_branch", [2, 4, 512, D])
    w_out = din("w_out", [2, D, D])
    pp_d = din("pp", [2, 128, PP_W])
    cc_d = din("cc", [128, CC_W])
    rope_d = din("rope", [128, 2048])
    feats_d = din("feats", [128, 1280])
    win_d = {256: din("win256", [256, 512]), 1024: din("win1024", [1024, 512])}
    fwd_d = {256: din("fwd256", [256, 512]), 1024: din("fwd1024", [1024, 2048])}
    inv_d = {256: din("inv256", [512, 256]), 1024: din("inv1024", [2048, 1024])}

    y_out = [dout("y_ctx", [T, D]), dout("y_lat", [T, D])]
    ndk = dout("ndk", [4, 2, 256, 512])
    ndv = dout("ndv", [4, 2, 256, 512])
    nwk = dout("nwk", [4, 2, 256, 128])
    nwv = dout("nwv", [4, 2, 256, 128])
    nrf = dout("nrf", [4, 2, 4, 64, 128])
    nrb = dout("nrb", [4, 2, 4, 64, 128])
    dbg_d = dout("dbg", [128, 16, T]) if dbg else None

    with ExitStack() as top:
        P = Prog(nc, top)

        _TINFO.clear()

        def alloc(stk, shape, dt, name="t"):
            uid[0] += 1
            t = stk.enter_context(nc.sbuf_tensor("%s_%d" % (name, uid[0]), list(shape), dt))
            _TINFO[t.name] = (int(nc.lookup_mloc(t).addr), 2 if dt == BF16 else 4)
            return t

        P.banks = [top.enter_context(nc.psum_tensor("psb%d" % i, [128, 512], F32)) for i in range(8)]
        FIX = P.banks[4:8]

        xT = alloc(top, [128, KD, T], F32, "xT")
        hT = alloc(top, [128, KD, T], BF16, "hT")
        slots = [alloc(top, [128, 8192], BF16, "wslot") for _ in range(2)]
        wslot = Ring(slots)
        pp = alloc(top, [128, 2, PP_W], F32, "pp")
        cc = alloc(top, [128, CC_W], F32, "cc")
        feats = alloc(top, [128, 1280], F32, "feats")
        modv = alloc(top, [128, 2, 72, 2], F32, "modv")
        Atab = alloc(top, [128, 2, 3, KD, 2], F32, "Atab")
        Gtab = alloc(top, [128, 2, 3, KD, 2], F32, "Gtab")
        cb16 = alloc(top, [128, 6, 128], BF16, "cb16")
        msk16 = alloc(top, [128, 2, 128], BF16, "msk16")
        onesf = alloc(top, [128, 128], F32, "onesf")
        sc = alloc(top, [128, 8], F32, "sc")
        lt = alloc(top, [128, 64], F32, "lt")
        DM = alloc(top, [128, 4, 128], F32, "DM")
        QD = alloc(top, [128, 2, 2, 128], F32, "QD")
        condT = alloc(top, [128, 16], F32, "condT")
        condb = alloc(top, [128, 16], BF16, "condb")
        tmpf = Ring([alloc(top, [128, 512], F32, "tmpf") for _ in range(4)])
        tmpb = Ring([alloc(top, [128, 512], BF16, "tmpb") for _ in range(6)])
        rsr = Ring([alloc(top, [128, 512], F32, "rsr") for _ in range(2)])
        ostage = Ring([alloc(top, [128, 1024], F32, "ostage") for _ in range(2)])
        small = Ring([alloc(top, [128, 16], F32, "small") for _ in range(4)])

        def ccv(name):
            o, w = CC_OFF[name]
            return cc[:, o:o + w]

        def ppv(l, name):
            o, w = PP_OFF[name]
            return pp[:, l, o:o + w]

        identf = ccv("ident")
        onesb = cb16[:, 0, :]
        ones1024b = cb16[:, 1, :]
        ones128b = cb16[:, 2, :]
        blk64b = cb16[:, 3, :]
        permb = cb16[:, 4, :]
        mle_b = msk16[:, 0, :]
        mge_b = msk16[:, 1, :]
        epsc = sc[:, 0:1]
        negpi = sc[:, 1:2]

        P.dma(pp[:], pp_d.rearrange("l p w -> p l w"))
        P.dma(cc[:], cc_d)
        P.dma(feats[:], feats_d)
        P.dma(condT[:], cond_d)
        P.memset(cb16[:, 0, :], 1.0)
        P.memset(cb16[:, 1, :], 1.0 / 1024.0)
        P.memset(cb16[:, 2, :], 1.0 / 128.0)
        P.memset(onesf[:], 1.0)
        P.memset(sc[:, 0:1], EPS)
        P.memset(sc[:, 1:2], -math.pi)
        P.copy(cb16[:, 3, :], ccv("blk64"))
        P.copy(cb16[:, 4, :], ccv("perm"))
        P.copy(msk16[:, 0, :], ccv("mle"))
        P.copy(msk16[:, 1, :], ccv("mge"))
        P.act(condb[:], condT[:], AF.Silu)
        condb3 = condb[:].rearrange("p (k g) -> p k g", g=2)

        def wview(slot, k, n, off=0):
            return slot[:, off:off + k * n].rearrange("p (k n) -> p k n", k=k)

        def wsrc(dram2d):
            return dram2d.rearrange("(k p) n -> p k n", p=128)

        ada_done = set()

        def ada_tables(l, s_):
            if (l, s_) in ada_done:
                return
            ada_done.add((l, s_))
            for blk in range(3 * s_, 3 * s_ + 3):
                s = wslot()
                wv = wview(s, 8, 1024)
                P.dma(wv, wsrc(w_ada[l][:, blk * 1024:(blk + 1) * 1024]), queue="pool")
                pst = P.rot()
                for m in range(8):
                    for k in range(8):
                        P.mm(pst[:, 2 * m:2 * m + 2], wv[:, k, m * 128:(m + 1) * 128], condb3[:, k, :],
                             start=(k == 0), stop=(k == 7))
                P.tt(modv[:, l, blk * 8:blk * 8 + 8, :], pst[:, 0:16].rearrange("p (m g) -> p m g", g=2),
                     ppv(l, "bada")[:, blk * 8:blk * 8 + 8].unsqueeze(2).to_broadcast([128, 8, 2]), ALU.add)
            gain = ppv(l, "g%d" % s_)
            scl = modv[:, l, 8 * (3 * s_ + 1):8 * (3 * s_ + 1) + 8, :]
            gat = modv[:, l, 8 * (3 * s_ + 2):8 * (3 * s_ + 2) + 8, :]
            for g_ in range(2):
                P.stt(Atab[:, l, s_, :, g_], scl[:, :, g_], 1.0, gain, ALU.add, ALU.mult)
            P.ts(Gtab[:, l, s_, :, :], gat, 0.5 if s_ != 1 else 1.0, ALU.mult)

        def Bsh(l, s_, k, g):
            return modv[:, l, 8 * (3 * s_) + k, g:g + 1]

        def load_x(g):
            for tc in range(8):
                st = ostage()
                P.dma(st[:], x_in[g][tc * 128:(tc + 1) * 128, :])
                for j in range(2):
                    pst = P.rot()
                    for q in range(4):
                        k = j * 4 + q
                        P.tr(pst[:, q * 128:(q + 1) * 128], st[:, k * 128:(k + 1) * 128], identf)
                    P.copy(xT[:, j * 4:j * 4 + 4, tc * 128:(tc + 1) * 128],
                           pst[:].rearrange("p (q t) -> p q t", q=4), eng=("act" if j else "dve"))

        def store_x(g):
            for tc in range(8):
                st = ostage()
                for j in range(2):
                    pst = P.rot()
                    for q in range(4):
                        k = j * 4 + q
                        P.tr(pst[:, q * 128:(q + 1) * 128], xT[:, k, tc * 128:(tc + 1) * 128], identf)
                    P.copy(st[:, j * 512:(j + 1) * 512], pst[:], eng=("act" if j else "dve"))
                P.dma(y_out[g][tc * 128:(tc + 1) * 128, :], st[:])

        def rstd_from(pn, n, scale=None):
            rs = rsr()
            P.act(rs[:, :n], pn, AF.Ln, bias=epsc, scale=scale)
            P.act(rs[:, :n], rs[:, :n], AF.Exp, scale=-0.5)
            return rs

        def rmsnorm_mod(l, s_, g):
            for half in range(2):
                cols = slice(half * 512, half * 512 + 512)
                pst = P.rot()
                for k in range(KD):
                    sq = tmpb()
                    P.act(sq[:], xT[:, k, cols], AF.Square)
                    P.mm(pst[:], ones1024b, sq[:], start=(k == 0), stop=(k == KD - 1))
                rs = rstd_from(pst[:], 512)
                for k in range(KD):
                    t1 = tmpf()
                    P.stt(t1[:], xT[:, k, cols], Atab[:, l, s_, k, g:g + 1], rs[:], ALU.mult, ALU.mult)
                    P.act(hT[:, k, cols], t1[:], AF.Identity, bias=Bsh(l, s_, k, g))

        def ffn(l, which, g):
            s_ = 0 if which == 0 else 2
            ada_tables(l, s_)
            rmsnorm_mod(l, s_, g)
            if dbg == "ffn":
                for k in range(8):
                    st = ostage()
                    P.copy(st[:], hT[:, k, :])
                    P.dma(dbg_d[:, k, :], st[:])
                st = ostage()
                P.memset(st[:], 0.0)
                P.copy(st[:, 0:288], modv[:].rearrange("p a b c -> p (a b c)"))
                P.copy(st[:, 288:384], Atab[:].rearrange("p a b c d -> p (a b c d)"))
                P.copy(st[:, 384:480], Gtab[:].rearrange("p a b c d -> p (a b c d)"))
                P.dma(dbg_d[:, 8, :], st[:])
            wi = w_ff_in[which][l]
            wo = w_ff_out[which][l]
            import os
            cut = os.environ.get("CUT") if l == 1 else None
            if cut == "rms":
                return
            with ExitStack() as ph:
                P.nrot = 8
                u = alloc(ph, [128, NJ, T], BF16, "ffu")
                ftmp = Ring([alloc(ph, [128, 512], F32, "ftmp") for _ in range(8)])
                for j0 in range(0, NJ, 4):
                    if cut and cut.startswith("in") and j0 >= int(cut[2:]):
                        break
                    nj = min(4, NJ - j0)
                    s = wslot()
                    wv = wview(s, 16, 512)
                    P.dma(wv[:, 0:8, 0:nj * 128], wsrc(wi[:, j0 * 128:(j0 + nj) * 128]), queue="pool")
                    P.dma(wv[:, 8:16, 0:nj * 128], wsrc(wi[:, DFF + j0 * 128:DFF + (j0 + nj) * 128]), queue="pool")
                    for jj in range(nj):
                        j = j0 + jj
                        for half in range(2):
                            cols = slice(half * 512, half * 512 + 512)
                            pa = P.rot()
                            pg = P.rot()
                            for k in range(KD):
                                P.mm(pa[:], wv[:, k, jj * 128:(jj + 1) * 128], hT[:, k, cols], start=(k == 0), stop=(k == KD - 1))
                            for k in range(KD):
                                P.mm(pg[:], wv[:, 8 + k, jj * 128:(jj + 1) * 128], hT[:, k, cols], start=(k == 0), stop=(k == KD - 1))
                            sa = ftmp()
                            P.act(sa[:], pa[:], AF.Silu)
                            P.tt(u[:, j, cols], sa[:], pg[:], ALU.mult)
                for mb in range(4):
                    if cut and (cut.startswith("in") or (cut.startswith("out") and mb >= int(cut[3:]))):
                        break
                    s = wslot()
                    wv = wview(s, NJ, 256)
                    P.dma(wv, wsrc(wo[:, mb * 256:(mb + 1) * 256]), queue="pool")
                    for mm_ in range(2):
                        m = mb * 2 + mm_
                        for half in range(2):
                            cols = slice(half * 512, half * 512 + 512)
                            pst = P.rot()
                            for j in range(NJ):
                                P.mm(pst[:], wv[:, j, mm_ * 128:(mm_ + 1) * 128], u[:, j, cols], start=(j == 0), stop=(j == NJ - 1))
                            P.stt(xT[:, m, cols], pst[:], Gtab[:, l, s_, m, g:g + 1], xT[:, m, cols], ALU.mult, ALU.add)
                P.nrot = 4
                P.barrier()

        def run_pipe(items, LA=2):
            pend = []
            for it in items:
                it[0]()
                it[1]()
                pend.append(it)
                if len(pend) > LA:
                    d = pend.pop(0)
                    d[2]()
                    if d[3]:
                        d[3]()
            for d in pend:
                d[2]()
                if d[3]:
                    d[3]()

        def proj_fm(wv, nchunks, cb):
            for ci in range(nchunks):
                for half in range(2):
                    cols = slice(half * 512, half * 512 + 512)
                    pst = P.rot()
                    for k in range(KD):
                        P.mm(pst[:], wv[:, k, ci * 128:(ci + 1) * 128], hT[:, k, cols], start=(k == 0), stop=(k == KD - 1))
                    cb(ci, half, cols, pst)

        def proj_tm(wv, ncols, cb, c0=0):
            for tk in range(8):
                pst = P.rot()
                for k in range(KD):
                    P.mm(pst[:, :ncols], hT[:, k, tk * 128:(tk + 1) * 128], wv[:, k, c0:c0 + ncols], start=(k == 0), stop=(k == KD - 1))
                cb(tk, pst)

        def qknorm_fm(pst, n, gaincol, outb, ropecols=None, rope=None, outf=None):
            outs = outb if isinstance(outb, list) else [(slice(0, 128), outb)]
            sq = tmpb()
            P.act(sq[:, :n], pst[:, :n], AF.Square)
            pn = P.rot()
            P.mm(pn[:, :n], blk64b, sq[:, :n])
            rs = rstd_from(pn[:, :n], n)
            if outf is not None:
                P.stt(outf, pst[:, :n], gaincol, rs[:, :n], ALU.mult, ALU.mult)
            if ropecols is None:
                for (psl, o) in outs:
                    P.stt(o, pst[psl, :n], gaincol[psl], rs[psl, :n], ALU.mult, ALU.mult)
                return
            xb = tmpb()
            P.stt(xb[:, :n], pst[:, :n], gaincol, rs[:, :n], ALU.mult, ALU.mult)
            pr = P.rot()
            P.mm(pr[:, :n], permb, xb[:, :n])
            t1 = tmpf()
            P.tt(t1[:, :n], xb[:, :n], rope[:, ropecols], ALU.mult)
            t2 = tmpf()
            P.tt(t2[:, :n], pr[:, :n], rope[:, 1024 + ropecols.start:1024 + ropecols.stop], ALU.mult)
            for (psl, o) in outs:
                P.tt(o, t1[psl, :n], t2[psl, :n], ALU.add)

        fixring = Ring(list(P.banks[4:8]))

        def proj_qk_pipe(wv, nchunks, gaincol, out_fn, rope=None, outf_fn=None):
            tiles = [(ci, half) for ci in range(nchunks) for half in range(2)]
            st = [dict() for _ in tiles]
            n = 512

            def s0(i):
                ci, half = tiles[i]
                cols = slice(half * 512, half * 512 + 512)
                pst = fixring()
                for k in range(KD):
                    P.mm(pst[:], wv[:, k, ci * 128:(ci + 1) * 128], hT[:, k, cols], start=(k == 0), stop=(k == KD - 1))
                sq = tmpb()
                P.act(sq[:], pst[:], AF.Square)
                st[i].update(pst=pst, sq=sq, cols=cols, ci=ci)

            def s1(i):
                d = st[i]
                pn = P.rot()
                P.mm(pn[:], blk64b, d["sq"][:])
                rs = rstd_from(pn[:], n)
                pst = d["pst"]
                outf = outf_fn(d["ci"], d["cols"]) if outf_fn else None
                if outf is not None:
                    P.stt(outf, pst[:], gaincol, rs[:], ALU.mult, ALU.mult)
                outb = out_fn(d["ci"], d["cols"])
                outs = outb if isinstance(outb, list) else [(slice(0, 128), outb)]
                if rope is None:
                    for (psl, o) in outs:
                        P.stt(o, pst[psl, :], gaincol[psl], rs[psl, :], ALU.mult, ALU.mult)
                    return
                xb = tmpb()
                P.stt(xb[:], pst[:], gaincol, rs[:], ALU.mult, ALU.mult)
                d.update(xb=xb, outs=outs)

            def s2(i):
                if rope is None:
                    return
                d = st[i]
                cols = d["cols"]
                pr = P.rot()
                P.mm(pr[:], permb, d["xb"][:])
                t1 = tmpf()
                P.tt(t1[:], d["xb"][:], rope[:, cols], ALU.mult)
                t2 = tmpf()
                P.tt(t2[:], pr[:], rope[:, 1024 + cols.start:1024 + cols.stop], ALU.mult)
                for (psl, o) in d["outs"]:
                    P.tt(o, t1[psl, :], t2[psl, :], ALU.add)

            nt = len(tiles)
            for i in range(nt + 2):
                if i < nt:
                    s0(i)
                if 0 <= i - 1 < nt:
                    s1(i - 1)
                if 0 <= i - 2 < nt:
                    s2(i - 2)

        def knorm_tm(pst, ng, gainrow, outf):
            nc_ = ng * 64
            sq = tmpf()
            P.act(sq[:, :nc_], pst[:, :nc_], AF.Square)
            ss = small()
            P.op("dve", lambda e: e.tensor_reduce(out=ss[:, :ng], in_=sq[:, :nc_].rearrange("p (g d) -> p g d", d=64),
                                                  axis=AX.X, op=ALU.add), reads=[sq[:, :nc_]], writes=[ss[:, :ng]])
            s2 = small()
            P.act(s2[:, :ng], ss[:, :ng], AF.Sqrt, bias=epsc, scale=1.0 / 64.0)
            P.recip(s2[:, :ng], s2[:, :ng])
            for gi in range(ng):
                P.stt(outf[:, gi * 64:(gi + 1) * 64], pst[:, gi * 64:(gi + 1) * 64], s2[:, gi:gi + 1], gainrow, ALU.mult, ALU.mult)

        def sin_layer(zout, w, kin, rhs, L, bcol):
            for c0 in range(0, L, 512):
                n = min(512, L - c0)
                pst = P.rot()
                P.mm(pst[:64, :n], w, rhs[:kin, c0:c0 + n])
                tu = tmpf()
                P.ts(tu[:64, :n], pst[:64, :n], bcol, ALU.add, 1.0 / (2.0 * math.pi), ALU.mult)
                ti = itmp
                P.ts(ti[:64, :n], tu[:64, :n], 8.5, ALU.add)
                tf_ = tmpf()
                P.copy(tf_[:64, :n], ti[:64, :n])
                P.stt(tu[:64, :n], tu[:64, :n], 8.5, tf_[:64, :n], ALU.add, ALU.subtract)
                P.ts(tf_[:64, :n], tu[:64, :n], 0.0, ALU.is_lt)
                P.tt(tu[:64, :n], tu[:64, :n], tf_[:64, :n], ALU.add)
                P.act(zout[:, c0:c0 + n], tu[:64, :n], AF.Sin, bias=negpi[:64], scale=2.0 * math.pi)

        itmp = alloc(top, [128, 512], I32, "itmp")

        def fwd_views(L):
            ntk = L // 128
            if L == 1024:
                sa = wslot()
                fa = wview(sa, 8, 1024)
                P.dma(fa, wsrc(fwd_d[L][:, 0:1024]), queue="pool")
                sb_ = wslot()
                fb = wview(sb_, 8, 1024)
                P.dma(fb, wsrc(fwd_d[L][:, 1024:2048]), queue="pool")
                return (lambda tk, i: fa[:, tk, i * 128:(i + 1) * 128]), (lambda tk, i: fb[:, tk, i * 128:(i + 1) * 128])
            s = wslot()
            f = wview(s, 2, 512)
            P.dma(f, wsrc(fwd_d[L]), queue="pool")
            return (lambda tk, i: f[:, tk, i * 128:(i + 1) * 128]), (lambda tk, i: f[:, tk, 256 + i * 128:256 + (i + 1) * 128])

        def filtergen(l, g, Ksp):
            L = 256 if g == 0 else 1024
            ntk = L // 128
            fo = 0 if g == 0 else 256
            with ExitStack() as ph:
                w1 = alloc(ph, [128, 64], F32, "hw1")
                w2 = alloc(ph, [128, 64], F32, "hw2")
                w3 = alloc(ph, [128, 1024], F32, "hw3")
                z1 = alloc(ph, [64, L], F32, "z1")
                z2 = alloc(ph, [64, L], F32, "z2")
                hp = alloc(ph, [128, ntk, 512], BF16, "hp")
                hm = alloc(ph, [128, ntk, 512], BF16, "hm")
                wst = Ring([alloc(ph, [128, 512], F32, "wst") for _ in range(2)])
                rn = alloc(ph, [128, 512], F32, "rn")
                P.dma(w1[:33, :], hy_w1[l])
                P.dma(w2[:64, :], hy_w2[l])
                P.dma(w3[:64, :], hy_w3[l])
                sin_layer(z1, w1[:33, :], 33, feats[:, fo:fo + L], L, ppv(l, "hb1")[:64])
                sin_layer(z2, w2[:64, :], 64, z1, L, ppv(l, "hb2")[:64])
                pn = FIX[0]
                for tk in range(ntk):
                    ws = wst()
                    P.dma(ws[:], win_d[L][tk * 128:(tk + 1) * 128, :])
                    hfb = []
                    for fb_ in range(2):
                        pst = P.rot()
                        P.mm(pst[:], z2[:, tk * 128:(tk + 1) * 128], w3[:64, fb_ * 512:(fb_ + 1) * 512])
                        h_ = tmpf()
                        P.tt(h_[:], pst[:], ws[:], ALU.mult)
                        ab = tmpf()
                        P.act(ab[:], h_[:], AF.Abs)
                        P.mm(pn[:], onesf[:], ab[:], start=(tk == 0 and fb_ == 0), stop=(tk == ntk - 1 and fb_ == 1))
                        hfb.append(h_)
                    if tk == 0:
                        P.memset(hfb[1][0:1, :], 0.0)
                    P.tt(hp[:, tk, :], hfb[0][:], hfb[1][:], ALU.add)
                    P.tt(hm[:, tk, :], hfb[0][:], hfb[1][:], ALU.subtract)
                P.recip(rn[:], pn[:])
                cosw, sinw = fwd_views(L)
                for i in range(ntk):
                    for cs, (wf, hsrc) in enumerate(((cosw, hp), (sinw, hm))):
                        pst = P.rot()
                        for tk in range(ntk):
                            P.mm(pst[:], wf(tk, i), hsrc[:, tk, :], start=(tk == 0), stop=(tk == ntk - 1))
                        P.tt(Ksp[:, cs, i, :], pst[:], rn[:], ALU.mult)
                P.barrier()

        def hyena(l, g, yb0, Ksp):
            L = 256 if g == 0 else 1024
            ntk = L // 128
            nseq = T // L
            W = w_in[l]
            cw = ppv(l, "cw").rearrange("p (c j) -> p c j", j=3)
            cbv = ppv(l, "cb")
            skip = ppv(l, "skip")
            with ExitStack() as ph:
                uT = alloc(ph, [128, 4, T], F32, "uT")
                with ExitStack() as ph2:
                    hst = Ring([alloc(ph2, [128, T], F32, "hst") for _ in range(2)])
                    ctmp = alloc(ph2, [128, T], F32, "ctmp")
                    for blk in range(3):
                        s = wslot()
                        wv = wview(s, 8, 512)
                        P.dma(wv, wsrc(W[:, blk * 512:(blk + 1) * 512]), queue="pool")
                        for c4 in range(4):
                            ci = blk * 4 + c4
                            hs = hst()
                            for half in range(2):
                                cols = slice(half * 512, half * 512 + 512)
                                pst = P.rot()
                                for k in range(KD):
                                    P.mm(pst[:], wv[:, k, c4 * 128:(c4 + 1) * 128], hT[:, k, cols], start=(k == 0), stop=(k == KD - 1))
                                P.copy(hs[:, cols], pst[:], eng="act")
                            cvo = uT[:, c4, :] if blk == 0 else ctmp[:]
                            P.ts(cvo, hs[:], cw[:, ci, 1:2], ALU.mult, cbv[:, ci:ci + 1], ALU.add)
                            c3 = cvo.rearrange("p (s t) -> p s t", s=nseq)
                            h3 = hs[:].rearrange("p (s t) -> p s t", s=nseq)
                            P.stt(c3[:, :, 1:L], h3[:, :, 0:L - 1], cw[:, ci, 0:1], c3[:, :, 1:L], ALU.mult, ALU.add)
                            P.stt(c3[:, :, 0:L - 1], h3[:, :, 1:L], cw[:, ci, 2:3], c3[:, :, 0:L - 1], ALU.mult, ALU.add)
                            if blk == 1:
                                P.copy(yb0[:, c4, :], ctmp[:], eng="act")
                            elif blk == 2:
                                P.tt(uT[:, c4, :], uT[:, c4, :], ctmp[:], ALU.mult)
                    P.barrier()
                u_tm = alloc(ph, [128, 8, 512], BF16, "u_tm")
                Y = alloc(ph, [128, 16, 512], BF16, "Yspec")
                for tk in range(8):
                    pst = P.rot()
                    for c4 in range(4):
                        P.tr(pst[:, c4 * 128:(c4 + 1) * 128], uT[:, c4, tk * 128:(tk + 1) * 128], identf)
                    P.copy(u_tm[:, tk, :], pst[:], eng="act")
                cosw, sinw = fwd_views(L)
                for sq_ in range(nseq):
                    for i in range(ntk):
                        pc = P.rot()
                        psn = P.rot()
                        for tk in range(ntk):
                            P.mm(pc[:], cosw(tk, i), u_tm[:, sq_ * ntk + tk, :], start=(tk == 0), stop=(tk == ntk - 1))
                        for tk in range(ntk):
                            P.mm(psn[:], sinw(tk, i), u_tm[:, sq_ * ntk + tk, :], start=(tk == 0), stop=(tk == ntk - 1))
                        Kc = Ksp[:, 0, i, :]
                        Ks = Ksp[:, 1, i, :]
                        t1 = tmpf()
                        t2 = tmpf()
                        P.tt(t1[:], pc[:], Kc, ALU.mult)
                        P.tt(t2[:], psn[:], Ks, ALU.mult)
                        P.tt(Y[:, sq_ * 2 * ntk + i, :], t1[:], t2[:], ALU.subtract)
                        t3 = tmpf()
                        t4 = tmpf()
                        P.tt(t3[:], psn[:], Kc, ALU.mult)
                        P.tt(t4[:], pc[:], Ks, ALU.mult)
                        P.tt(Y[:, sq_ * 2 * ntk + ntk + i, :], t3[:], t4[:], ALU.add)

                def epilogue(c4, cols, pv, n):
                    t = tmpf()
                    P.stt(t[:, :n], uT[:, c4, cols], skip[:, c4:c4 + 1], pv, ALU.mult, ALU.add)
                    P.tt(yb0[:, c4, cols], t[:, :n], yb0[:, c4, cols], ALU.mult)

                if L == 1024:
                    for th in range(2):
                        s = wslot()
                        iv = wview(s, 16, 512)
                        P.dma(iv, wsrc(inv_d[L][:, th * 512:(th + 1) * 512]), queue="pool")
                        for c4 in range(4):
                            pst = P.rot()
                            for j in range(16):
                                P.mm(pst[:], Y[:, j, c4 * 128:(c4 + 1) * 128], iv[:, j, :], start=(j == 0), stop=(j == 15))
                            epilogue(c4, slice(th * 512, th * 512 + 512), pst[:], 512)
                else:
                    s = wslot()
                    iv = wview(s, 4, 256)
                    P.dma(iv, wsrc(inv_d[L]), queue="pool")
                    for sq_ in range(nseq):
                        for c4 in range(4):
                            pst = P.rot()
                            for j in range(4):
                                P.mm(pst[:, :256], Y[:, sq_ * 4 + j, c4 * 128:(c4 + 1) * 128], iv[:, j, :], start=(j == 0), stop=(j == 3))
                            epilogue(c4, slice(sq_ * 256, sq_ * 256 + 256), pst[:, :256], 256)
                P.barrier()

        def layer_tables(l):
            dl = ppv(l, "dlam")
            a = small()
            pr = tmpf()
            P.tt(pr[:, 0:64], dl[:, 0:64], dl[:, 64:128], ALU.mult)
            P.tt(pr[:, 64:128], dl[:, 128:192], dl[:, 192:256], ALU.mult)
            P.op("dve", lambda e: e.tensor_reduce(out=a[:, 0:2], in_=pr[:, 0:128].rearrange("p (g d) -> p g d", d=64), axis=AX.X, op=ALU.add),
                 reads=[pr[:, 0:128]], writes=[a[:, 0:2]])
            P.act(a[:, 2:4], a[:, 0:2], AF.Exp)
            lam_init = 0.8 - 0.6 * math.exp(-0.3 * l)
            P.stt(lt[:, 0:1], a[:, 3:4], -lam_init, a[:, 2:3], ALU.add, ALU.subtract)
            P.ts(lt[:, 38:39], ppv(l, "dsub"), 1.0 - lam_init, ALU.mult)
            P.act(lt[:, 2:10], ppv(l, "sink"), AF.Exp)
            for (src, dst) in ((ppv(l, "rdf"), lt[:, 10:14]), (ppv(l, "rdb"), lt[:, 14:18])):
                e_ = small()
                P.act(e_[:, 0:4], src, AF.Exp, scale=-1.0)
                P.ts(e_[:, 0:4], e_[:, 0:4], 1.0, ALU.add)
                P.act(e_[:, 4:8], e_[:, 0:4], AF.Ln)
                P.ts(dst, e_[:, 4:8], -1.0, ALU.mult)
            P.ts(lt[:, 34:38], lt[:, 14:18], -1.0, ALU.mult)
            for c in range(2):
                for hh in range(2):
                    h = 2 * c + hh
                    ps_ = slice(64 * hh, 64 * hh + 64)
                    P.copy(lt[ps_, 18 + c:19 + c], lt[ps_, 10 + h:11 + h])
                    P.copy(lt[ps_, 20 + c:21 + c], lt[ps_, 14 + h:15 + h])
            for h in range(4):
                P.act(lt[:, 22 + h:23 + h], ccv("cola"), AF.Exp, scale=lt[:, 10 + h:11 + h])
                P.act(lt[:, 26 + h:27 + h], ccv("colb"), AF.Exp, scale=lt[:, 14 + h:15 + h])
                d1 = tmpf()
                d2 = tmpf()
                P.act(d1[:, :128], ccv("rel"), AF.Exp, scale=lt[:, 10 + h:11 + h])
                P.tt(d1[:, :128], d1[:, :128], ccv("mle"), ALU.mult)
                P.act(d2[:, :128], ccv("rel"), AF.Exp, scale=lt[:, 34 + h:35 + h])
                P.tt(d2[:, :128], d2[:, :128], ccv("mge"), ALU.mult)
                P.tt(DM[:, h, :], d1[:, :128], d2[:, :128], ALU.add)
            for c in range(2):
                P.act(QD[:, 0, c, :], ccv("iota1"), AF.Exp, scale=lt[:, 18 + c:19 + c])
                P.act(QD[:, 1, c, :], ccv("iotar"), AF.Exp, scale=lt[:, 20 + c:21 + c])
                P.act(lt[:, 30 + c:31 + c], lt[:, 18 + c:19 + c], AF.Exp, scale=128.0)
                P.act(lt[:, 32 + c:33 + c], lt[:, 20 + c:21 + c], AF.Exp, scale=128.0)

        def diff_attn(l, g, yb1):
            W = w_in[l]
            nk = T + (512 if g == 1 else 0)
            nvc = 8 + (4 if g == 1 else 0)
            rp = (lambda cols: cols) if g == 1 else (lambda cols: None)
            with ExitStack() as ph:
                rope = None
                if g == 1:
                    rope = alloc(ph, [128, 2048], F32, "rope")
                    P.dma(rope[:], rope_d)
                qm = alloc(ph, [128, 2, 4, T], BF16, "dqm")
                P.memset(qm[0:64, 1, :, :], 0.0, eng="pool")
                P.memset(qm[64:128, 0, :, :], 0.0, eng="pool")
                kT = alloc(ph, [128, 4, nk], BF16, "dkT")
                v_tm = alloc(ph, [128, nvc, 512], BF16, "dv")
                s = wslot()
                wv = wview(s, 8, 512)
                P.dma(wv, wsrc(W[:, 1536:2048]), queue="pool")
                proj_qk_pipe(wv, 4, ppv(l, "dqn"),
                             lambda ci, cols: [(slice(0, 64), qm[0:64, 0, ci, cols]), (slice(64, 128), qm[64:128, 1, ci, cols])], rope)
                s = wslot()
                wv = wview(s, 8, 512)
                P.dma(wv, wsrc(W[:, 2048:2560]), queue="pool")
                knf = alloc(ph, [128, 4, T], F32, "knf") if g == 0 else None
                proj_qk_pipe(wv, 4, ppv(l, "dkn"), lambda ci, cols: kT[:, ci, cols], rope,
                             outf_fn=(lambda ci, cols: knf[:, ci, cols]) if g == 0 else None)
                import os
                if g == 0 and not os.environ.get("NODIFFK"):
                    for tk in range(8):
                        pst = P.rot()
                        for ci in range(4):
                            P.tr(pst[:, ci * 128:(ci + 1) * 128], knf[:, ci, tk * 128:(tk + 1) * 128], identf)
                        ost = ostage()
                        P.copy(ost[:, 0:512], pst[:], eng="act")
                        P.dma(ndk[tk // 2, l, (tk % 2) * 128:(tk % 2) * 128 + 128, :], ost[:, 0:512])
                s = wslot()
                wv = wview(s, 8, 512)
                P.dma(wv, wsrc(W[:, 2560:3072]), queue="pool")

                def vcb(tk, pst):
                    P.copy(v_tm[:, tk, :], pst[:], eng="act")
                    if g == 0:
                        ost = ostage()
                        P.copy(ost[:, 0:512], pst[:])
                        P.dma(ndv[tk // 2, l, (tk % 2) * 128:(tk % 2) * 128 + 128, :], ost[:, 0:512])
                proj_tm(wv, 512, vcb)
                if g == 1:
                    for tk in range(4):
                        st = ostage()
                        P.dma(st[:, 0:512], cdk[l, tk * 128:(tk + 1) * 128, :])
                        pst = P.rot()
                        for h in range(4):
                            P.tr(pst[:, h * 128:(h + 1) * 128], st[:, h * 128:(h + 1) * 128], identf)
                        P.copy(kT[:, :, T + tk * 128:T + (tk + 1) * 128], pst[:].rearrange("p (h t) -> p h t", h=4))
                    P.dma(v_tm[:, 8:12, :], wsrc(cdv[l]), queue="pool")
                nlam = lt[:, 0:1]
                gsub = lt[:, 38:39]
                osb = [alloc(ph, [128, 512], F32, "osb") for _ in range(2)]
                if g == 0:
                    jobs = [(sq_ * 256, 256, [(sq_ * 256 + j * 128, sq_ * 2 + j) for j in range(2)]) for sq_ in range(4)]
                else:
                    kl = [(j * 128, j) for j in range(8)] + [(T + j * 128, 8 + j) for j in range(4)]
                    jobs = [(0, 512, kl), (512, 512, kl)]
                items = []
                for (q0, n, kl) in jobs:
                    for h in range(4):
                        for c in range(2):
                            for ji, (k0, vch) in enumerate(kl):
                                st = {}

                                def fS(st=st, h=h, c=c, k0=k0, q0=q0, n=n):
                                    st["pS"] = P.rot()
                                    P.mm(st["pS"][:, :n], kT[:, h, k0:k0 + 128], qm[:, c, h, q0:q0 + n])

                                def fE(st=st, n=n):
                                    st["pT"] = tmpb()
                                    P.act(st["pT"][:, :n], st["pS"][:, :n], AF.Exp, scale=0.125)

                                def fP(st=st, h=h, c=c, vch=vch, n=n, first=(ji == 0), last=(ji == len(kl) - 1)):
                                    P.mm(FIX[2 * c][:, :n], v_tm[:, vch, h * 128:(h + 1) * 128], st["pT"][:, :n], start=first, stop=last)
                                    P.mm(FIX[2 * c + 1][:, :n], onesb, st["pT"][:, :n], start=first, stop=last)

                                fEnd = None
                                if ji == len(kl) - 1:
                                    def fEnd(h=h, c=c, q0=q0, n=n):
                                        r = tmpf()
                                        P.recip(r[:, :n], FIX[2 * c + 1][:, :n])
                                        P.tt(osb[c][:, :n], FIX[2 * c][:, :n], r[:, :n], ALU.mult)
                                        if c == 1:
                                            y = tmpf()
                                            P.stt(y[:, :n], osb[1][:, :n], nlam, osb[0][:, :n], ALU.mult, ALU.add)
                                            sq = tmpb()
                                            P.act(sq[:, :n], y[:, :n], AF.Square)
                                            pn = P.rot()
                                            P.mm(pn[:, :n], ones128b, sq[:, :n])
                                            rs = rstd_from(pn[:, :n], n)
                                            P.stt(yb1[:, h, q0:q0 + n], y[:, :n], gsub, rs[:, :n], ALU.mult, ALU.mult)
                                items.append((fS, fE, fP, fEnd))
                run_pipe(items)
                P.barrier()

        def win_attn(l, g, yb2):
            W = w_in[l]
            nk = T + (512 if g == 1 else 0)
            nvc = 8 + (4 if g == 1 else 0)
            rp = (lambda cols: cols) if g == 1 else (lambda cols: None)
            with ExitStack() as ph:
                rope = None
                if g == 1:
                    rope = alloc(ph, [128, 2048], F32, "rope")
                    P.dma(rope[:], rope_d)
                qwm = alloc(ph, [128, 2, 4, T], BF16, "wqm")
                P.memset(qwm[0:64, 1, :, :], 0.0, eng="pool")
                P.memset(qwm[64:128, 0, :, :], 0.0, eng="pool")
                kw2 = alloc(ph, [128, 2, nk], BF16, "wk2")
                vw = alloc(ph, [128, nvc, 256], BF16, "wv")
                s = wslot()
                wv = wview(s, 8, 512)
                P.dma(wv, wsrc(W[:, 3072:3584]), queue="pool")
                proj_qk_pipe(wv, 4, ppv(l, "wqn"),
                             lambda ci, cols: [(slice(0, 64), qwm[0:64, 0, ci, cols]), (slice(64, 128), qwm[64:128, 1, ci, cols])], rope)
                s = wslot()
                wk = wview(s, 8, 512)
                for hk in range(2):
                    for dup in range(2):
                        P.dma(wk[:, :, (hk * 2 + dup) * 64:(hk * 2 + dup + 1) * 64], wsrc(W[:, 3584 + hk * 64:3584 + (hk + 1) * 64]), queue="pool")
                P.dma(wk[:, :, 256:384], wsrc(W[:, 3584:3712]), queue="pool")
                P.dma(wk[:, :, 384:512], wsrc(W[:, 3712:3840]), queue="pool")
                knf = alloc(ph, [128, 2, T], F32, "wknf") if g == 0 else None
                proj_qk_pipe(wk, 2, ppv(l, "wkn"), lambda ci, cols: kw2[:, ci, cols], rope,
                             outf_fn=(lambda ci, cols: knf[:, ci, cols]) if g == 0 else None)
                import os
                if g == 0 and not os.environ.get("NOWINK"):
                    for tk in range(8):
                        pst = P.rot()
                        for ci in range(2):
                            P.tr(pst[:, ci * 128:(ci + 1) * 128], knf[:, ci, tk * 128:(tk + 1) * 128], identf)
                        ost = ostage()
                        P.copy(ost[:, 0:256], pst[:, 0:256], eng="act")
                        r0 = (tk % 2) * 128
                        P.dma(nwk[tk // 2, l, r0:r0 + 128, 0:64], ost[:, 0:64])
                        P.dma(nwk[tk // 2, l, r0:r0 + 128, 64:128], ost[:, 192:256])

                def vcb(tk, pst):
                    v4 = vw[:, tk, :].rearrange("p (h u d) -> p h u d", h=2, u=2)
                    p3 = pst[:, 0:128].rearrange("p (h d) -> p h d", h=2)
                    P.copy(v4[:, :, 0, :], p3, eng="act")
                    P.copy(v4[:, :, 1, :], p3, eng="act")
                    if g == 0:
                        ost = ostage()
                        P.copy(ost[:, 0:128], pst[:, 0:128])
                        P.dma(nwv[tk // 2, l, (tk % 2) * 128:(tk % 2) * 128 + 128, :], ost[:, 0:128])
                proj_tm(wk, 128, vcb, c0=384)
                if g == 1:
                    st = ostage()
                    st5 = st[:].rearrange("p (k h u d) -> p k h u d", k=4, h=2, u=2)
                    src = cwk[l].rearrange("(k p) (h d) -> p k h d", p=128, h=2)
                    for dup in range(2):
                        for tk in range(4):
                            P.dma(st5[:, tk, :, dup, :], src[:, tk])
                    st3 = st[:].rearrange("p (k n) -> p k n", k=4)
                    for tk in range(4):
                        pst = P.rot()
                        for hk in range(2):
                            P.tr(pst[:, hk * 128:(hk + 1) * 128], st3[:, tk, hk * 128:(hk + 1) * 128], identf)
                        P.copy(kw2[:, :, T + tk * 128:T + (tk + 1) * 128], pst[:, 0:256].rearrange("p (h t) -> p h t", h=2))
                    vsrc = cwv[l].rearrange("(k p) (h d) -> p k h d", p=128, h=2)
                    for tk in range(4):
                        v4 = vw[:, 8 + tk, :].rearrange("p (h u d) -> p h u d", h=2, u=2)
                        for dup in range(2):
                            P.dma(v4[:, :, dup, :], vsrc[:, tk], queue="pool")
                esink = lt[:, 2:10]
                items = []
                gcount = [0]

                def mk_finish(h, q0, n, pO, pZ):
                    def fin():
                        po = slice(64 * (h % 2), 64 * (h % 2) + 64)
                        r = tmpf()
                        P.ts(r[po, :n], pZ[po, :n], esink[po, h:h + 1], ALU.add)
                        P.recip(r[po, :n], r[po, :n])
                        P.tt(yb2[po, h // 2, q0:q0 + n], pO[po, :n], r[po, :n], ALU.mult)
                    return fin

                def add_item(h, kcols, qcols, vch, n, cs, first, last, mask, pO, pZ, fEnd):
                    hk = h // 4
                    c4 = h // 2
                    vc = slice(hk * 128, hk * 128 + 128)
                    st = {}

                    def fS():
                        st["pS"] = P.rot()
                        P.mm(st["pS"][:, :n], kw2[:, hk, kcols], qwm[:, h % 2, c4, qcols])

                    def fE():
                        st["pT"] = tmpb()
                        P.act(st["pT"][:, :n], st["pS"][:, :n], AF.Exp, scale=0.125)
                        if mask is not None:
                            P.tt(st["pT"][:, :n], st["pT"][:, :n], mask, ALU.mult)

                    def fP():
                        P.mm(pO[:, cs], vw[:, vch, vc], st["pT"][:, :n], start=first, stop=last)
                        P.mm(pZ[:, cs], onesb, st["pT"][:, :n], start=first, stop=last)
                    items.append((fS, fE, fP, fEnd))

                for h in range(8):
                    if g == 0:
                        for sq_ in range(4):
                            q0 = sq_ * 256
                            pO, pZ = FIX[2 * (gcount[0] % 2)], FIX[2 * (gcount[0] % 2) + 1]
                            gcount[0] += 1
                            for j in range(2):
                                add_item(h, slice(q0 + j * 128, q0 + (j + 1) * 128), slice(q0, q0 + 256), sq_ * 2 + j, 256, slice(0, 256),
                                         j == 0, j == 1, None, pO, pZ, mk_finish(h, q0, 256, pO, pZ) if j == 1 else None)
                    else:
                        for half in range(2):
                            q0 = half * 512
                            pO, pZ = FIX[2 * (gcount[0] % 2)], FIX[2 * (gcount[0] % 2) + 1]
                            gcount[0] += 1
                            for j in range(4):
                                add_item(h, slice(T + j * 128, T + (j + 1) * 128), slice(q0, q0 + 512), 8 + j, 512, slice(0, 512),
                                         j == 0, False, None, pO, pZ, None)
                            for qb in range(4):
                                nb = half * 4 + qb
                                qc0 = nb * 128
                                cs = slice(qb * 128, qb * 128 + 128)
                                kbs = [kb for kb in (nb - 1, nb, nb + 1) if 0 <= kb <= 7]
                                for kb in kbs:
                                    last = (qb == 3 and kb == kbs[-1])
                                    mask = mge_b if kb == nb - 1 else (mle_b if kb == nb + 1 else None)
                                    add_item(h, slice(kb * 128, (kb + 1) * 128), slice(qc0, qc0 + 128), kb, 128, cs,
                                             False, last, mask, pO, pZ, mk_finish(h, q0, 512, pO, pZ) if last else None)
                run_pipe(items)
                P.barrier()

        def retention(l, g, yb3):
            W = w_in[l]
            L = 256 if g == 0 else 1024
            ntk = L // 128
            nseq = T // L
            with ExitStack() as ph:
                rq = alloc(ph, [128, 2, T], BF16, "rq")
                rk = alloc(ph, [128, 2, T], BF16, "rk")
                rk_tm = alloc(ph, [128, 8, 256], BF16, "rk_tm")
                rv = alloc(ph, [128, 8, 512], BF16, "rv")
                rg = alloc(ph, [128, 4, T], BF16, "rg")
                Qd = alloc(ph, [128, 2, 2, T], BF16, "Qd")
                Kd = alloc(ph, [128, 2, 8, 256], BF16, "Kd")
                SB = alloc(ph, [128, 2, 2, 8, 128], BF16, "SB")
                Srun = alloc(ph, [128, 2, 2, 128], F32, "Srun")
                s = wslot()
                wv = wview(s, 8, 512)
                P.dma(wv, wsrc(W[:, 3840:4352]), queue="pool")

                def qkcb(ci, half, cols, pst):
                    if ci < 2:
                        P.copy(rq[:, ci, cols], pst[:], eng="act")
                    else:
                        P.act(rk[:, ci - 2, cols], pst[:], AF.Copy, scale=0.125)
                proj_fm(wv, 4, qkcb)
                proj_tm(wv, 256, lambda tk, pst: P.act(rk_tm[:, tk, :], pst[:, 0:256], AF.Copy, scale=0.125), c0=256)
                s = wslot()
                wv = wview(s, 8, 512)
                P.dma(wv, wsrc(W[:, 4352:4864]), queue="pool")
                proj_tm(wv, 512, lambda tk, pst: P.copy(rv[:, tk, :], pst[:], eng="act"))
                s = wslot()
                wv = wview(s, 8, 512)
                P.dma(wv, wsrc(W[:, 4864:5376]), queue="pool")
                proj_fm(wv, 4, lambda ci, half, cols, pst: P.act(rg[:, ci, cols], pst[:], AF.Silu))
                for d_ in range(2):
                    for c in range(2):
                        for k8 in range(8):
                            P.tt(Qd[:, d_, c, k8 * 128:(k8 + 1) * 128], rq[:, c, k8 * 128:(k8 + 1) * 128], QD[:, d_, c, :], ALU.mult)
                    for h in range(4):
                        P.ts(Kd[:, d_, :, 64 * h:64 * h + 64], rk_tm[:, :, 64 * h:64 * h + 64], lt[:, 22 + 4 * d_ + h:23 + 4 * d_ + h], ALU.mult)
                for sq_ in range(nseq):
                    tb = sq_ * ntk
                    if g == 1:
                        for d_, src in ((0, srf), (1, srb)):
                            P.dma(Srun[:, d_, :, :], src[l].rearrange("(c hh) d e -> (hh d) c e", hh=2))
                    else:
                        P.memset(Srun[:], 0.0)
                    for d_ in range(2):
                        order = list(range(ntk)) if d_ == 0 else list(range(ntk - 1, -1, -1))
                        for n_ in order:
                            gn = tb + n_
                            for c in range(2):
                                P.copy(SB[:, d_, c, gn, :], Srun[:, d_, c, :], eng="pool")
                                pD = P.rot()
                                for hh in range(2):
                                    h = 2 * c + hh
                                    P.mm(pD[64 * hh:64 * hh + 64, 0:128], Kd[:, d_, gn, 64 * h:64 * h + 64], rv[:, gn, 128 * h:128 * h + 128])
                                P.stt(Srun[:, d_, c, :], Srun[:, d_, c, :], lt[:, 30 + 2 * d_ + c:31 + 2 * d_ + c], pD[:, 0:128], ALU.mult, ALU.add)
                    if g == 0:
                        for d_, dst in ((0, nrf), (1, nrb)):
                            P.dma(dst[sq_, l].rearrange("(c hh) d e -> (hh d) c e", hh=2), Srun[:, d_, :, :])
                atr = Ring([alloc(ph, [128, 128], BF16, "atr") for _ in range(8)])
                items = []
                for h in range(4):
                    for half in range(2):
                        for qb in range(4):
                            st = {}

                            def fS(st=st, h=h, half=half, qb=qb):
                                c = h // 2
                                ps_ = slice(64 * (h % 2), 64 * (h % 2) + 64)
                                gn = half * 4 + qb
                                tcs = slice(gn * 128, gn * 128 + 128)
                                st["pS"] = P.rot()
                                P.mm(st["pS"][:, :128], rk[ps_, c, tcs], rq[ps_, c, tcs])

                            def fE(st=st, h=h):
                                st["At"] = atr()
                                P.tt(st["At"][:, :128], st["pS"][:, :128], DM[:, h, :], ALU.mult)

                            def fP(st=st, h=h, half=half, qb=qb):
                                c = h // 2
                                ps_ = slice(64 * (h % 2), 64 * (h % 2) + 64)
                                gn = half * 4 + qb
                                cs = slice(qb * 128, qb * 128 + 128)
                                tcs = slice(gn * 128, gn * 128 + 128)
                                pOut = FIX[(h * 2 + half) % 4]
                                P.mm(pOut[:, cs], rv[:, gn, 128 * h:128 * h + 128], st["At"][:, :128], start=True, stop=False)
                                P.mm(pOut[:, cs], SB[ps_, 0, c, gn, :], Qd[ps_, 0, c, tcs], start=False, stop=False)
                                P.mm(pOut[:, cs], SB[ps_, 1, c, gn, :], Qd[ps_, 1, c, tcs], start=False, stop=True)

                            fEnd = None
                            if qb == 3:
                                def fEnd(h=h, half=half):
                                    pOut = FIX[(h * 2 + half) % 4]
                                    cols = slice(half * 512, half * 512 + 512)
                                    y = tmpf()
                                    P.copy(y[:], pOut[:], eng="act")
                                    sq = tmpb()
                                    P.act(sq[:], y[:], AF.Square)
                                    pn = P.rot()
                                    P.mm(pn[:], ones128b, sq[:])
                                    rs = rstd_from(pn[:], 512)
                                    t = tmpf()
                                    P.tt(t[:], y[:], rs[:], ALU.mult)
                                    P.tt(yb3[:, h, cols], t[:], rg[:, h, cols], ALU.mult)
                            items.append((fS, fE, fP, fEnd))
                run_pipe(items)
                P.barrier()

        def merge(l, g, yb):
            W = w_in[l]
            with ExitStack() as ph:
                P.nrot = 8
                mg = alloc(ph, [128, KD, T], BF16, "merged")
                mtmp = Ring([alloc(ph, [128, 512], F32, "mtmp") for _ in range(12)])
                for m in range(KD):
                    sa = wslot()
                    wb = sa[:, 0:2048].rearrange("p (b k n) -> p b k n", b=4, k=4)
                    wg = sa[:, 2048:6144].rearrange("p (b k n) -> p b k n", b=4, k=8)
                    for b in range(4):
                        P.dma(wb[:, b], wsrc(w_branch[l, b][:, m * 128:(m + 1) * 128]), queue="pool")
                    for b in range(4):
                        P.dma(wg[:, b], wsrc(W[:, 5376 + b * 1024 + m * 128:5376 + b * 1024 + (m + 1) * 128]), queue="pool")
                    for half in range(2):
                        cols = slice(half * 512, half * 512 + 512)
                        acc = None
                        for b in range(4):
                            pB = P.rot()
                            for kc in range(4):
                                P.mm(pB[:], wb[:, b, kc, :], yb[b][:, kc, cols], start=(kc == 0), stop=(kc == 3))
                            pG = P.rot()
                            for k in range(KD):
                                P.mm(pG[:], wg[:, b, k, :], hT[:, k, cols], start=(k == 0), stop=(k == KD - 1))
                            sg = mtmp()
                            P.act(sg[:], pG[:], AF.Sigmoid)
                            t = mtmp()
                            P.tt(t[:], pB[:], sg[:], ALU.mult)
                            if b == 0:
                                acc = t
                            elif b < 3:
                                t2 = mtmp()
                                P.tt(t2[:], t[:], acc[:], ALU.add)
                                acc = t2
                            else:
                                P.tt(mg[:, m, cols], t[:], acc[:], ALU.add)
                s = wslot()
                wo = wview(s, 8, 1024)
                P.dma(wo, wsrc(w_out[l]), queue="pool")
                for m in range(KD):
                    for half in range(2):
                        cols = slice(half * 512, half * 512 + 512)
                        pst = P.rot()
                        for k in range(KD):
                            P.mm(pst[:], wo[:, k, m * 128:(m + 1) * 128], mg[:, k, cols], start=(k == 0), stop=(k == KD - 1))
                        P.stt(xT[:, m, cols], pst[:], Gtab[:, l, 1, m, g:g + 1], xT[:, m, cols], ALU.mult, ALU.add)
                P.nrot = 4
                P.barrier()

        def mixer(l, g):
            ada_tables(l, 1)
            rmsnorm_mod(l, 1, g)
            layer_tables(l)
            if dbg == "tables":
                st = ostage()
                P.memset(st[:], 0.0)
                P.copy(st[:, 0:64], lt[:])
                P.copy(st[:, 64:576], DM[:].rearrange("p h i -> p (h i)"))
                P.copy(st[:, 576:1088 - 64], QD[:].rearrange("p d c i -> p (d c i)")[:, 0:448])
                P.dma(dbg_d[:, 0, :], st[:])
            with ExitStack() as ms:
                yb = [None] * 4
                import os
                mc = int(os.environ.get("MIXCUT", "9"))
                yb[3] = alloc(ms, [128, 4, T], BF16, "yb3")
                if mc >= 1:
                    retention(l, g, yb[3])
                yb[0] = alloc(ms, [128, 4, T], BF16, "yb0")
                with ExitStack() as ks:
                    Ksp = alloc(ks, [128, 2, 8, 512], BF16, "Ksp")
                    if mc >= 2:
                        filtergen(l, g, Ksp)
                    if mc >= 3:
                        hyena(l, g, yb[0], Ksp)
                yb[1] = alloc(ms, [128, 4, T], BF16, "yb1")
                if mc >= 4:
                    diff_attn(l, g, yb[1])
                yb[2] = alloc(ms, [128, 4, T], BF16, "yb2")
                if mc >= 5:
                    win_attn(l, g, yb[2])
                if mc < 6:
                    return
                if dbg and dbg == (l, g):
                    for b in range(4):
                        for c4 in range(4):
                            st = ostage()
                            P.copy(st[:], yb[b][:, c4, :])
                            P.dma(dbg_d[:, b * 4 + c4, :], st[:])
                merge(l, g, yb)

        stages = []
        for g in groups:
            stages.append(("load", g, 0))
            for l in range(2):
                stages.append(("ffa", g, l))
                stages.append(("mix", g, l))
                stages.append(("ffb", g, l))
            stages.append(("store", g, 0))
        for (kind, g, l) in stages:
            if kind == "load":
                load_x(g)
            elif kind == "store":
                store_x(g)
            elif kind == "ffa":
                ffn(l, 0, g)
            elif kind == "ffb":
                ffn(l, 1, g)
            elif kind == "mix":
                mixer(l, g)
            if stop_after is not None and (kind, g, l) == tuple(stop_after):
                if kind != "store":
                    store_x(g)
                break
        P.finish()
        print("program: ninst=%d nwaits=%d" % (P.ninst, P.nwaits), "counts", P.count, "dmax", {q: max(v) for q, v in P.dcount.items()})
    return nc


_CACHE = {}


def _get_program(stop_after=None, dbg=False):
    key = (tuple(stop_after) if stop_after else None, dbg)
    if key not in _CACHE:
        _CACHE[key] = build_program(stop_after=stop_after, dbg=dbg)
    return _CACHE[key]


def make_in_maps(inp):
    f = lambda a: np.ascontiguousarray(np.asarray(a, dtype=np.float32))
    consts = _host_consts()
    pp = _host_params(inp)
    shared = {
        "w_ada": f(inp["w_ada"]), "w_ffa_in": f(inp["w_ffa_in"]), "w_ffb_in": f(inp["w_ffb_in"]),
        "w_ffa_out": f(inp["w_ffa_out"]), "w_ffb_out": f(inp["w_ffb_out"]), "w_in": f(inp["w_in"]),
        "hy_f_w1": f(inp["hy_f_w1"]), "hy_f_w2": f(inp["hy_f_w2"]), "hy_f_w3": f(inp["hy_f_w3"]),
        "w_branch": f(inp["w_branch"]), "w_out": f(inp["w_out"]), "pp": pp,
    }
    for k, v in consts.items():
        shared[k] = v
    maps = []
    for i in range(NCORES):
        m = dict(shared)
        m["x_ctx"] = f(inp["x_prompt"][4 * i:4 * i + 4]).reshape(T, D)
        m["x_lat"] = f(inp["x_sample"][i]).reshape(T, D)
        cond = np.stack([f(inp["c_ctx"]).reshape(8, 128).T, f(inp["c"][i]).reshape(8, 128).T], axis=2)
        m["cond"] = np.ascontiguousarray(cond.reshape(128, 16))
        m["cdk"] = f(inp["cache_diff_k"][i]).reshape(2, 512, 512)
        m["cdv"] = f(inp["cache_diff_v"][i]).reshape(2, 512, 512)
        m["cwk"] = f(inp["cache_win_k"][i]).reshape(2, 512, 128)
        m["cwv"] = f(inp["cache_win_v"][i]).reshape(2, 512, 128)
        m["srf"] = f(inp["state_ret_f"][i])
        m["srb"] = f(inp["state_ret_b"][i])
        maps.append(m)
    return maps


def kernel(**inputs):
    nc = _get_program()
    maps = make_in_maps(inputs)
    res = run_bass_kernel_spmd(nc, maps, core_ids=list(range(NCORES)))
    R = res.results
    y_prompt = np.concatenate([R[i]["y_ctx"].reshape(4, 256, D) for i in range(NCORES)], axis=0)
    y_sample = np.stack([R[i]["y_lat"] for i in range(NCORES)], axis=0)
    ndk = np.concatenate([R[i]["ndk"] for i in range(NCORES)], axis=0).reshape(32, 2, 256, 4, 2, 64)
    ndv = np.concatenate([R[i]["ndv"] for i in range(NCORES)], axis=0).reshape(32, 2, 256, 4, 128)
    nwk = np.concatenate([R[i]["nwk"] for i in range(NCORES)], axis=0).reshape(32, 2, 256, 2, 64)
    nwv = np.concatenate([R[i]["nwv"] for i in range(NCORES)], axis=0).reshape(32, 2, 256, 2, 64)
    nrf = np.concatenate([R[i]["nrf"] for i in range(NCORES)], axis=0)
    nrb = np.concatenate([R[i]["nrb"] for i in range(NCORES)], axis=0)
    return tuple(np.ascontiguousarray(a.astype(np.float32)) for a in (y_prompt, y_sample, ndk, ndv, nwk, nwv, nrf, nrb))
```

```python
import math
from contextlib import ExitStack

import numpy as np
import concourse.bass as bass
import concourse.mybir as mybir
from concourse.bass_utils import run_bass_kernel_spmd

F32 = mybir.dt.float32
BF16 = mybir.dt.bfloat16
I32 = mybir.dt.int32
AF = mybir.ActivationFunctionType
ALU = mybir.AluOpType
AX = mybir.AxisListType

NCORES = 8
import os as _os
_NOSELFW = bool(_os.environ.get('NOSELFW'))
_BARRIERS = bool(_os.environ.get('BARRIERS'))
D = 1024
KD = 8
DFF = 2816
NJ = 22
T = 1024
NDMASEM = 12
EPS = 1e-6


def _isap(x):
    return hasattr(x, "tensor") and hasattr(x, "ap")


_TINFO = {}


def _region(ap):
    t = ap.tensor
    pat = ap.ap
    pstride = pat[0][0]
    off = ap.offset
    if pstride == 0:
        pstride = 1 << 40
    p_lo = off // pstride
    f_lo = off % pstride
    p_hi = p_lo + pat[0][1]
    ext = 1
    for st, cnt in pat[1:]:
        ext += abs(st) * (cnt - 1)
    ti = _TINFO.get(t.name)
    if ti is not None:
        return ("SB", p_lo, p_hi, ti[0] + f_lo * ti[1], ti[0] + (f_lo + ext) * ti[1])
    return (t.name, p_lo, p_hi, f_lo, f_lo + ext)


_PAGE = 2048


def _keys(name, fl, fh):
    if name != "SB":
        return (name,)
    return [("SB", pg) for pg in range(fl // _PAGE, (fh - 1) // _PAGE + 1)]


class Prog:
    def __init__(self, nc, stack):
        self.nc = nc
        self.eng = {"pe": nc.tensor, "act": nc.scalar, "dve": nc.vector, "pool": nc.gpsimd, "sp": nc.sync}
        self.sems = {}
        self.count = {}
        for e in self.eng:
            self.sems[e] = stack.enter_context(nc.semaphore("s_" + e))
            self.count[e] = 0
        self.dsem = {}
        self.dcount = {}
        self.dnext = {}
        for q in ("sp", "pool"):
            self.dsem[q] = [stack.enter_context(nc.semaphore("d_%s%d" % (q, i))) for i in range(NDMASEM)]
            self.dcount[q] = [0] * NDMASEM
            self.dnext[q] = 0
        self.seen = {e: {} for e in self.eng}
        self.snap = {}
        self.recs = {}
        self.nwaits = 0
        self.ninst = 0
        self.banks = []
        self.rot_i = 0
        self.nrot = 4

    def _deps(self, reads, writes, engine):
        need = {}
        for ap in reads:
            name, pl, ph, fl, fh = _region(ap)
            psum = name != "SB"
            for key in _keys(name, fl, fh):
                for r in self.recs.get(key, ()):
                    if psum:
                        if r[5] != engine and r[0] < ph and pl < r[1]:
                            if need.get(r[5], 0) < r[6]:
                                need[r[5]] = r[6]
                    if r[4] == "W" and r[0] < ph and pl < r[1] and r[2] < fh and fl < r[3]:
                        if need.get(r[5], 0) < r[6]:
                            need[r[5]] = r[6]
        for ap in writes:
            name, pl, ph, fl, fh = _region(ap)
            if name != "SB":
                fl, fh = 0, 1 << 30
            for key in _keys(name, fl, fh):
                for r in self.recs.get(key, ()):
                    if r[0] < ph and pl < r[1] and r[2] < fh and fl < r[3]:
                        if r[5] == engine and (engine == "pe" or _NOSELFW):
                            continue
                        if need.get(r[5], 0) < r[6]:
                            need[r[5]] = r[6]
        return need

    def _record(self, reads, writes, key, val):
        for ap in writes:
            name, pl, ph, fl, fh = _region(ap)
            rec = [pl, ph, fl, fh, "W", key, val]
            for k in _keys(name, fl, fh):
                lst = self.recs.setdefault(k, [])
                lst[:] = [r for r in lst if not (pl <= r[0] and r[1] <= ph and fl <= r[2] and r[3] <= fh)]
                lst.append(rec)
        for ap in reads:
            name, pl, ph, fl, fh = _region(ap)
            rec = [pl, ph, fl, fh, "R", key, val]
            for k in _keys(name, fl, fh):
                lst = self.recs.setdefault(k, [])
                lst[:] = [r for r in lst if not (r[4] == "R" and r[5] == key and pl <= r[0] and r[1] <= ph
                                                 and fl <= r[2] and r[3] <= fh)]
                lst.append(rec)

    def _semh(self, key):
        if isinstance(key, str):
            return self.sems[key]
        return self.dsem[key[0]][key[1]]

    def _wait(self, engine, need):
        seen = self.seen[engine]
        h = self.eng[engine]
        for k, v in need.items():
            if seen.get(k, 0) >= v:
                continue
            h.wait_ge(self._semh(k), v)
            self.nwaits += 1
            sn = self.snap.get((k, v))
            if sn:
                for kk, vv in sn.items():
                    if seen.get(kk, 0) < vv:
                        seen[kk] = vv
            seen[k] = v

    @staticmethod
    def _onchip(ap):
        return "DRam" not in ap.tensor.__class__.__name__

    def op(self, engine, fn, reads=(), writes=()):
        reads = [a for a in reads if _isap(a) and self._onchip(a)]
        writes = [a for a in writes if _isap(a) and self._onchip(a)]
        need = self._deps(reads, writes, engine)
        self._wait(engine, need)
        ins = fn(self.eng[engine])
        self.count[engine] += 1
        c = self.count[engine]
        ins.then_inc(self.sems[engine], 1)
        sn = dict(self.seen[engine])
        sn[engine] = c
        self.snap[(engine, c)] = sn
        self._record(reads, writes, engine, c)
        self.ninst += 1

    def dma(self, out, in_, queue="sp"):
        reads = [in_] if self._onchip(in_) else []
        writes = [out] if self._onchip(out) else []
        need = self._deps(reads, writes, None)
        i = self.dnext[queue]
        self.dnext[queue] = (i + 1) % NDMASEM
        key = (queue, i)
        prev = self.dcount[queue][i]
        if prev:
            need[key] = max(need.get(key, 0), prev)
        self._wait(queue, need)
        ins = self.eng[queue].dma_start(out=out, in_=in_)
        val = prev + 16
        ins.then_inc(self.dsem[queue][i], 16)
        self.dcount[queue][i] = val
        self.snap[(key, val)] = dict(self.seen[queue])
        self._record(reads, writes, key, val)
        self.ninst += 1

    def barrier(self):
        if not _BARRIERS:
            return
        need = {}
        for q in self.dsem:
            for i, v in enumerate(self.dcount[q]):
                if v:
                    need[(q, i)] = v
        for e in self.eng:
            if self.count[e]:
                need[e] = self.count[e]
        for e in self.eng:
            self._wait(e, {k: v for k, v in need.items() if k != e})
        self.recs = {}

    def finish(self):
        need = {}
        for q in self.dsem:
            for i, v in enumerate(self.dcount[q]):
                if v:
                    need[(q, i)] = v
        for e in self.eng:
            if e != "sp" and self.count[e]:
                need[e] = self.count[e]
        self._wait("sp", need)

    def rot(self):
        b = self.banks[self.rot_i % self.nrot]
        self.rot_i += 1
        return b

    def mm(self, out, lhsT, rhs, start=True, stop=True):
        self.op("pe", lambda e: e.matmul(out, lhsT=lhsT, rhs=rhs, start=start, stop=stop),
                reads=[lhsT, rhs], writes=[out])

    def tr(self, out, in_, ident):
        self.op("pe", lambda e: e.transpose(out, in_, ident), reads=[in_, ident], writes=[out])

    def act(self, out, in_, func, bias=None, scale=None):
        kw = {}
        if bias is not None:
            kw["bias"] = bias
        if scale is not None:
            kw["scale"] = scale
        self.op("act", lambda e: e.activation(out=out, in_=in_, func=func, **kw),
                reads=[in_, bias, scale], writes=[out])

    def copy(self, out, in_, eng="dve"):
        if eng == "act":
            self.op("act", lambda e: e.activation(out=out, in_=in_, func=AF.Copy), reads=[in_], writes=[out])
        else:
            self.op(eng, lambda e: e.tensor_copy(out=out, in_=in_), reads=[in_], writes=[out])

    def tt(self, out, a, b, op, eng="dve"):
        self.op(eng, lambda e: e.tensor_tensor(out=out, in0=a, in1=b, op=op), reads=[a, b], writes=[out])

    def ts(self, out, a, s1, op0, s2=None, op1=None, eng="dve"):
        if s2 is None:
            self.op(eng, lambda e: e.tensor_single_scalar(out=out, in_=a, scalar=s1, op=op0),
                    reads=[a, s1], writes=[out])
        else:
            self.op(eng, lambda e: e.tensor_scalar(out=out, in0=a, scalar1=s1, scalar2=s2, op0=op0, op1=op1),
                    reads=[a, s1, s2], writes=[out])

    def stt(self, out, a, s, b, op0, op1):
        self.op("dve", lambda e: e.scalar_tensor_tensor(out=out, in0=a, scalar=s, in1=b, op0=op0, op1=op1),
                reads=[a, s, b], writes=[out])

    def recip(self, out, in_):
        self.op("dve", lambda e: e.reciprocal(out=out, in_=in_), reads=[in_], writes=[out])

    def memset(self, ap, val, eng="dve"):
        self.op(eng, lambda e: e.memset(ap, val), writes=[ap])


class Ring:
    def __init__(self, items):
        self.items = items
        self.i = 0

    def __call__(self):
        x = self.items[self.i % len(self.items)]
        self.i += 1
        return x


PP_FIELDS = [("g0", 8), ("g1", 8), ("g2", 8), ("bada", 72), ("cw", 36), ("cb", 12), ("hb1", 1), ("hb2", 1),
             ("skip", 4), ("dqn", 1), ("dkn", 1), ("dsub", 1), ("wqn", 1), ("wkn", 1), ("dknrow", 64),
             ("wknrow", 64), ("dlam", 256), ("sink", 8), ("rdf", 4), ("rdb", 4)]
PP_OFF = {}
_o = 0
for _n, _w in PP_FIELDS:
    PP_OFF[_n] = (_o, _w)
    _o += _w
PP_W = _o

CC_FIELDS = [("ident", 128), ("blk64", 128), ("perm", 128), ("rel", 128), ("mle", 128), ("mge", 128),
             ("iota1", 128), ("iotar", 128), ("cola", 1), ("colb", 1)]
CC_OFF = {}
_o = 0
for _n, _w in CC_FIELDS:
    CC_OFF[_n] = (_o, _w)
    _o += _w
CC_W = _o


def _host_consts():
    p = np.arange(128)[:, None].astype(np.float64)
    i = np.arange(128)[None, :].astype(np.float64)
    cc = np.zeros((128, CC_W), np.float64)

    def put(name, arr):
        o, w = CC_OFF[name]
        cc[:, o:o + w] = arr

    put("ident", (p == i))
    put("blk64", ((p // 64) == (i // 64)) / 64.0)
    partner = np.where((np.arange(128) % 32) < 16, np.arange(128) + 16, np.arange(128) - 16)
    perm = np.zeros((128, 128))
    perm[partner, np.arange(128)] = 1.0
    put("perm", perm)
    put("rel", i - p)
    put("mle", (p <= i))
    put("mge", (p >= i))
    put("iota1", np.broadcast_to(i + 1.0, (128, 128)))
    put("iotar", np.broadcast_to(128.0 - i, (128, 128)))
    put("cola", 127.0 - p)
    put("colb", p)
    t = np.arange(1024)
    dd = np.arange(128) % 64
    half = dd // 32
    j = dd % 16
    inv_freq = 10000.0 ** (-j / 16.0)
    pos = np.where(half[:, None] == 0, (t // 64)[None, :], (t % 64)[None, :]).astype(np.float64)
    ang = pos * inv_freq[:, None]
    sgn = np.where((np.arange(128) % 32) < 16, -1.0, 1.0)[:, None]
    rope = np.concatenate([np.cos(ang), sgn * np.sin(ang)], axis=1)
    out = {"cc": cc.astype(np.float32), "rope": rope.astype(np.float32)}
    min_decay = math.log(1e-2) / 1.5
    max_decay = math.log(1e-2) / 0.3
    deltas = np.abs(np.linspace(min_decay, max_decay, 512))
    feats_all = np.zeros((128, 1280), np.float64)
    o = 0
    for L in (256, 1024):
        tt_ = np.arange(L) / L
        f = np.arange(1, 17)
        a = 2.0 * math.pi * tt_[:, None] * f[None, :]
        feats = np.concatenate([tt_[:, None], np.sin(a), np.cos(a)], axis=1)
        feats_all[:33, o:o + L] = feats.T
        o += L
        out["win%d" % L] = np.exp(-tt_[:, None] * deltas[None, :]).astype(np.float32)
        th = math.pi * (2.0 * np.arange(L)[None, :] + 1.0) * np.arange(L)[:, None] / (2.0 * L)
        fwd = np.concatenate([np.cos(th), np.sin(th)], axis=1)
        out["fwd%d" % L] = fwd.astype(np.float32)
        out["inv%d" % L] = (fwd.T / L).astype(np.float32)
    out["feats"] = feats_all.astype(np.float32)
    return out


def _host_params(inp):
    pp = np.zeros((2, 128, PP_W), np.float32)

    def put(l, name, arr):
        o, w = PP_OFF[name]
        pp[l][:, o:o + w] = np.asarray(arr, np.float32).reshape(128, w)

    for l in range(2):
        put(l, "g0", inp["norm_ffa"][l].reshape(8, 128).T)
        put(l, "g1", inp["norm_mix"][l].reshape(8, 128).T)
        put(l, "g2", inp["norm_ffb"][l].reshape(8, 128).T)
        put(l, "bada", inp["b_ada"][l].reshape(72, 128).T)
        put(l, "cw", inp["hy_conv_w"][l].reshape(3, 12, 128).transpose(2, 1, 0).reshape(128, 36))
        put(l, "cb", inp["hy_conv_b"][l].reshape(12, 128).T)
        hb = np.zeros((128, 1), np.float32)
        hb[:64, 0] = inp["hy_f_b1"][l]
        put(l, "hb1", hb)
        hb = np.zeros((128, 1), np.float32)
        hb[:64, 0] = inp["hy_f_b2"][l]
        put(l, "hb2", hb)
        put(l, "skip", inp["hy_skip"][l].reshape(4, 128).T)
        put(l, "dqn", np.tile(inp["diff_q_norm"][l], 2)[:, None])
        put(l, "dkn", np.tile(inp["diff_k_norm"][l], 2)[:, None])
        put(l, "dsub", inp["diff_subln"][l][:, None])
        put(l, "wqn", np.tile(inp["win_q_norm"][l], 2)[:, None])
        put(l, "wkn", np.tile(inp["win_k_norm"][l], 2)[:, None])
        put(l, "dknrow", np.broadcast_to(inp["diff_k_norm"][l][None, :], (128, 64)))
        put(l, "wknrow", np.broadcast_to(inp["win_k_norm"][l][None, :], (128, 64)))
        put(l, "dlam", np.broadcast_to(inp["diff_lambda"][l].reshape(1, 256), (128, 256)))
        put(l, "sink", np.broadcast_to(inp["win_sink"][l][None, :], (128, 8)))
        put(l, "rdf", np.broadcast_to(inp["ret_decay_f"][l][None, :], (128, 4)))
        put(l, "rdb", np.broadcast_to(inp["ret_decay_b"][l][None, :], (128, 4)))
    return pp


def build_program(stop_after=None, dbg=False, groups=(0, 1)):
    nc = bass.Bass("TRN2", target_bir_lowering=False)
    uid = [0]

    def din(name, shape):
        return nc.dram_tensor(name, list(shape), F32, kind="ExternalInput").ap()

    def dout(name, shape):
        return nc.dram_tensor(name, list(shape), F32, kind="ExternalOutput").ap()

    x_in = [din("x_ctx", [T, D]), din("x_lat", [T, D])]
    cond_d = din("cond", [128, 16])
    cdk = din("cdk", [2, 512, 512])
    cdv = din("cdv", [2, 512, 512])
    cwk = din("cwk", [2, 512, 128])
    cwv = din("cwv", [2, 512, 128])
    srf = din("srf", [2, 4, 64, 128])
    srb = din("srb", [2, 4, 64, 128])
    w_ada = din("w_ada", [2, D, 9 * D])
    w_ff_in = [din("w_ffa_in", [2, D, 2 * DFF]), din("w_ffb_in", [2, D, 2 * DFF])]
    w_ff_out = [din("w_ffa_out", [2, DFF, D]), din("w_ffb_out", [2, DFF, D])]
    w_in = din("w_in", [2, D, 9472])
    hy_w1 = din("hy_f_w1", [2, 33, 64])
    hy_w2 = din("hy_f_w2", [2, 64, 64])
    hy_w3 = din("hy_f_w3", [2, 64, 1024])
    w_branch = din("w_branch", [2, 4, 512, D])
    w_out = din("w_out", [2, D, D])
    pp_d = din("pp", [2, 128, PP_W])
    cc_d = din("cc", [128, CC_W])
    rope_d = din("rope", [128, 2048])
    feats_d = din("feats", [128, 1280])
    win_d = {256: din("win256", [256, 512]), 1024: din("win1024", [1024, 512])}
    fwd_d = {256: din("fwd256", [256, 512]), 1024: din("fwd1024", [1024, 2048])}
    inv_d = {256: din("inv256", [512, 256]), 1024: din("inv1024", [2048, 1024])}

    y_out = [dout("y_ctx", [T, D]), dout("y_lat", [T, D])]
    ndk = dout("ndk", [4, 2, 256, 512])
    ndv = dout("ndv", [4, 2, 256, 512])
    nwk = dout("nwk", [4, 2, 256, 128])
    nwv = dout("nwv", [4, 2, 256, 128])
    nrf = dout("nrf", [4, 2, 4, 64, 128])
    nrb = dout("nrb", [4, 2, 4, 64, 128])
    dbg_d = dout("dbg", [128, 16, T]) if dbg else None

    with ExitStack() as top:
        P = Prog(nc, top)

        _TINFO.clear()

        def alloc(stk, shape, dt, name="t"):
            uid[0] += 1
            t = stk.enter_context(nc.sbuf_tensor("%s_%d" % (name, uid[0]), list(shape), dt))
            _TINFO[t.name] = (int(nc.lookup_mloc(t).addr), 2 if dt == BF16 else 4)
            return t

        P.banks = [top.enter_context(nc.psum_tensor("psb%d" % i, [128, 512], F32)) for i in range(8)]
        FIX = P.banks[4:8]

        xT = alloc(top, [128, KD, T], F32, "xT")
        hT = alloc(top, [128, KD, T], BF16, "hT")
        slots = [alloc(top, [128, 8192], BF16, "wslot") for _ in range(2)]
        wslot = Ring(slots)
        pp = alloc(top, [128, 2, PP_W], F32, "pp")
        cc = alloc(top, [128, CC_W], F32, "cc")
        feats = alloc(top, [128, 1280], F32, "feats")
        modv = alloc(top, [128, 2, 72, 2], F32, "modv")
        Atab = alloc(top, [128, 2, 3, KD, 2], F32, "Atab")
        Gtab = alloc(top, [128, 2, 3, KD, 2], F32, "Gtab")
        cb16 = alloc(top, [128, 6, 128], BF16, "cb16")
        msk16 = alloc(top, [128, 2, 128], BF16, "msk16")
        onesf = alloc(top, [128, 128], F32, "onesf")
        sc = alloc(top, [128, 8], F32, "sc")
        lt = alloc(top, [128, 64], F32, "lt")
        DM = alloc(top, [128, 4, 128], F32, "DM")
        QD = alloc(top, [128, 2, 2, 128], F32, "QD")
        condT = alloc(top, [128, 16], F32, "condT")
        condb = alloc(top, [128, 16], BF16, "condb")
        tmpf = Ring([alloc(top, [128, 512], F32, "tmpf") for _ in range(4)])
        tmpb = Ring([alloc(top, [128, 512], BF16, "tmpb") for _ in range(6)])
        rsr = Ring([alloc(top, [128, 512], F32, "rsr") for _ in range(2)])
        ostage = Ring([alloc(top, [128, 1024], F32, "ostage") for _ in range(2)])
        small = Ring([alloc(top, [128, 16], F32, "small") for _ in range(4)])

        def ccv(name):
            o, w = CC_OFF[name]
            return cc[:, o:o + w]

        def ppv(l, name):
            o, w = PP_OFF[name]
            return pp[:, l, o:o + w]

        identf = ccv("ident")
        onesb = cb16[:, 0, :]
        ones1024b = cb16[:, 1, :]
        ones128b = cb16[:, 2, :]
        blk64b = cb16[:, 3, :]
        permb = cb16[:, 4, :]
        mle_b = msk16[:, 0, :]
        mge_b = msk16[:, 1, :]
        epsc = sc[:, 0:1]
        negpi = sc[:, 1:2]

        P.dma(pp[:], pp_d.rearrange("l p w -> p l w"))
        P.dma(cc[:], cc_d)
        P.dma(feats[:], feats_d)
        P.dma(condT[:], cond_d)
        P.memset(cb16[:, 0, :], 1.0)
        P.memset(cb16[:, 1, :], 1.0 / 1024.0)
        P.memset(cb16[:, 2, :], 1.0 / 128.0)
        P.memset(onesf[:], 1.0)
        P.memset(sc[:, 0:1], EPS)
        P.memset(sc[:, 1:2], -math.pi)
        P.copy(cb16[:, 3, :], ccv("blk64"))
        P.copy(cb16[:, 4, :], ccv("perm"))
        P.copy(msk16[:, 0, :], ccv("mle"))
        P.copy(msk16[:, 1, :], ccv("mge"))
        P.act(condb[:], condT[:], AF.Silu)
        condb3 = condb[:].rearrange("p (k g) -> p k g", g=2)

        def wview(slot, k, n, off=0):
            return slot[:, off:off + k * n].rearrange("p (k n) -> p k n", k=k)

        def wsrc(dram2d):
            return dram2d.rearrange("(k p) n -> p k n", p=128)

        ada_done = set()

        def ada_tables(l, s_):
            if (l, s_) in ada_done:
                return
            ada_done.add((l, s_))
            for blk in range(3 * s_, 3 * s_ + 3):
                s = wslot()
                wv = wview(s, 8, 1024)
                P.dma(wv, wsrc(w_ada[l][:, blk * 1024:(blk + 1) * 1024]), queue="pool")
                pst = P.rot()
                for m in range(8):
                    for k in range(8):
                        P.mm(pst[:, 2 * m:2 * m + 2], wv[:, k, m * 128:(m + 1) * 128], condb3[:, k, :],
                             start=(k == 0), stop=(k == 7))
                P.tt(modv[:, l, blk * 8:blk * 8 + 8, :], pst[:, 0:16].rearrange("p (m g) -> p m g", g=2),
                     ppv(l, "bada")[:, blk * 8:blk * 8 + 8].unsqueeze(2).to_broadcast([128, 8, 2]), ALU.add)
            gain = ppv(l, "g%d" % s_)
            scl = modv[:, l, 8 * (3 * s_ + 1):8 * (3 * s_ + 1) + 8, :]
            gat = modv[:, l, 8 * (3 * s_ + 2):8 * (3 * s_ + 2) + 8, :]
            for g_ in range(2):
                P.stt(Atab[:, l, s_, :, g_], scl[:, :, g_], 1.0, gain, ALU.add, ALU.mult)
            P.ts(Gtab[:, l, s_, :, :], gat, 0.5 if s_ != 1 else 1.0, ALU.mult)

        def Bsh(l, s_, k, g):
            return modv[:, l, 8 * (3 * s_) + k, g:g + 1]

        def load_x(g):
            for tc in range(8):
                st = ostage()
                P.dma(st[:], x_in[g][tc * 128:(tc + 1) * 128, :])
                for j in range(2):
                    pst = P.rot()
                    for q in range(4):
                        k = j * 4 + q
                        P.tr(pst[:, q * 128:(q + 1) * 128], st[:, k * 128:(k + 1) * 128], identf)
                    P.copy(xT[:, j * 4:j * 4 + 4, tc * 128:(tc + 1) * 128],
                           pst[:].rearrange("p (q t) -> p q t", q=4), eng=("act" if j else "dve"))

        def store_x(g):
            for tc in range(8):
                st = ostage()
                for j in range(2):
                    pst = P.rot()
                    for q in range(4):
                        k = j * 4 + q
                        P.tr(pst[:, q * 128:(q + 1) * 128], xT[:, k, tc * 128:(tc + 1) * 128], identf)
                    P.copy(st[:, j * 512:(j + 1) * 512], pst[:], eng=("act" if j else "dve"))
                P.dma(y_out[g][tc * 128:(tc + 1) * 128, :], st[:])

        def rstd_from(pn, n, scale=None):
            rs = rsr()
            P.act(rs[:, :n], pn, AF.Ln, bias=epsc, scale=scale)
            P.act(rs[:, :n], rs[:, :n], AF.Exp, scale=-0.5)
            return rs

        def rmsnorm_mod(l, s_, g):
            for half in range(2):
                cols = slice(half * 512, half * 512 + 512)
                pst = P.rot()
                for k in range(KD):
                    sq = tmpb()
                    P.act(sq[:], xT[:, k, cols], AF.Square)
                    P.mm(pst[:], ones1024b, sq[:], start=(k == 0), stop=(k == KD - 1))
                rs = rstd_from(pst[:], 512)
                for k in range(KD):
                    t1 = tmpf()
                    P.stt(t1[:], xT[:, k, cols], Atab[:, l, s_, k, g:g + 1], rs[:], ALU.mult, ALU.mult)
                    P.act(hT[:, k, cols], t1[:], AF.Identity, bias=Bsh(l, s_, k, g))

        def ffn(l, which, g):
            s_ = 0 if which == 0 else 2
            ada_tables(l, s_)
            rmsnorm_mod(l, s_, g)
            if dbg == "ffn":
                for k in range(8):
                    st = ostage()
                    P.copy(st[:], hT[:, k, :])
                    P.dma(dbg_d[:, k, :], st[:])
                st = ostage()
                P.memset(st[:], 0.0)
                P.copy(st[:, 0:288], modv[:].rearrange("p a b c -> p (a b c)"))
                P.copy(st[:, 288:384], Atab[:].rearrange("p a b c d -> p (a b c d)"))
                P.copy(st[:, 384:480], Gtab[:].rearrange("p a b c d -> p (a b c d)"))
                P.dma(dbg_d[:, 8, :], st[:])
            wi = w_ff_in[which][l]
            wo = w_ff_out[which][l]
            import os
            cut = os.environ.get("CUT") if l == 1 else None
            if cut == "rms":
                return
            with ExitStack() as ph:
                u = alloc(ph, [128, NJ, T], BF16, "ffu")
                ftmp = Ring([alloc(ph, [128, 512], F32, "ftmp") for _ in range(8)])
                for j0 in range(0, NJ, 4):
                    if cut and cut.startswith("in") and j0 >= int(cut[2:]):
                        break
                    nj = min(4, NJ - j0)
                    s = wslot()
                    wv = wview(s, 16, 512)
                    P.dma(wv[:, 0:8, 0:nj * 128], wsrc(wi[:, j0 * 128:(j0 + nj) * 128]), queue="pool")
                    P.dma(wv[:, 8:16, 0:nj * 128], wsrc(wi[:, DFF + j0 * 128:DFF + (j0 + nj) * 128]), queue="pool")
                    for jj in range(nj):
                        j = j0 + jj
                        for half in range(2):
                            cols = slice(half * 512, half * 512 + 512)
                            pa = P.rot()
                            pg = P.rot()
                            for k in range(KD):
                                P.mm(pa[:], wv[:, k, jj * 128:(jj + 1) * 128], hT[:, k, cols], start=(k == 0), stop=(k == KD - 1))
                            for k in range(KD):
                                P.mm(pg[:], wv[:, 8 + k, jj * 128:(jj + 1) * 128], hT[:, k, cols], start=(k == 0), stop=(k == KD - 1))
                            sa = ftmp()
                            P.act(sa[:], pa[:], AF.Silu)
                            P.tt(u[:, j, cols], sa[:], pg[:], ALU.mult)
                for mb in range(4):
                    if cut and (cut.startswith("in") or (cut.startswith("out") and mb >= int(cut[3:]))):
                        break
                    s = wslot()
                    wv = wview(s, NJ, 256)
                    P.dma(wv, wsrc(wo[:, mb * 256:(mb + 1) * 256]), queue="pool")
                    for mm_ in range(2):
                        m = mb * 2 + mm_
                        for half in range(2):
                            cols = slice(half * 512, half * 512 + 512)
                            pst = P.rot()
                            for j in range(NJ):
                                P.mm(pst[:], wv[:, j, mm_ * 128:(mm_ + 1) * 128], u[:, j, cols], start=(j == 0), stop=(j == NJ - 1))
                            P.stt(xT[:, m, cols], pst[:], Gtab[:, l, s_, m, g:g + 1], xT[:, m, cols], ALU.mult, ALU.add)
                P.barrier()

        def run_pipe(items, LA=2):
            pend = []
            for it in items:
                it[0]()
                it[1]()
                pend.append(it)
                if len(pend) > LA:
                    d = pend.pop(0)
                    d[2]()
                    if d[3]:
                        d[3]()
            for d in pend:
                d[2]()
                if d[3]:
                    d[3]()

        def proj_fm(wv, nchunks, cb):
            for ci in range(nchunks):
                for half in range(2):
                    cols = slice(half * 512, half * 512 + 512)
                    pst = P.rot()
                    for k in range(KD):
                        P.mm(pst[:], wv[:, k, ci * 128:(ci + 1) * 128], hT[:, k, cols], start=(k == 0), stop=(k == KD - 1))
                    cb(ci, half, cols, pst)

        def proj_tm(wv, ncols, cb, c0=0):
            for tk in range(8):
                pst = P.rot()
                for k in range(KD):
                    P.mm(pst[:, :ncols], hT[:, k, tk * 128:(tk + 1) * 128], wv[:, k, c0:c0 + ncols], start=(k == 0), stop=(k == KD - 1))
                cb(tk, pst)

        def qknorm_fm(pst, n, gaincol, outb, ropecols=None, rope=None, outf=None):
            outs = outb if isinstance(outb, list) else [(slice(0, 128), outb)]
            sq = tmpb()
            P.act(sq[:, :n], pst[:, :n], AF.Square)
            pn = P.rot()
            P.mm(pn[:, :n], blk64b, sq[:, :n])
            rs = rstd_from(pn[:, :n], n)
            if outf is not None:
                P.stt(outf, pst[:, :n], gaincol, rs[:, :n], ALU.mult, ALU.mult)
            if ropecols is None:
                for (psl, o) in outs:
                    P.stt(o, pst[psl, :n], gaincol[psl], rs[psl, :n], ALU.mult, ALU.mult)
                return
            xb = tmpb()
            P.stt(xb[:, :n], pst[:, :n], gaincol, rs[:, :n], ALU.mult, ALU.mult)
            pr = P.rot()
            P.mm(pr[:, :n], permb, xb[:, :n])
            t1 = tmpf()
            P.tt(t1[:, :n], xb[:, :n], rope[:, ropecols], ALU.mult)
            t2 = tmpf()
            P.tt(t2[:, :n], pr[:, :n], rope[:, 1024 + ropecols.start:1024 + ropecols.stop], ALU.mult)
            for (psl, o) in outs:
                P.tt(o, t1[psl, :n], t2[psl, :n], ALU.add)

        fixring = Ring(list(P.banks[4:8]))

        def proj_qk_pipe(wv, nchunks, gaincol, out_fn, rope=None, outf_fn=None):
            tiles = [(ci, half) for ci in range(nchunks) for half in range(2)]
            st = [dict() for _ in tiles]
            n = 512

            def s0(i):
                ci, half = tiles[i]
                cols = slice(half * 512, half * 512 + 512)
                pst = fixring()
                for k in range(KD):
                    P.mm(pst[:], wv[:, k, ci * 128:(ci + 1) * 128], hT[:, k, cols], start=(k == 0), stop=(k == KD - 1))
                sq = tmpb()
                P.act(sq[:], pst[:], AF.Square)
                st[i].update(pst=pst, sq=sq, cols=cols, ci=ci)

            def s1(i):
                d = st[i]
                pn = P.rot()
                P.mm(pn[:], blk64b, d["sq"][:])
                rs = rstd_from(pn[:], n)
                pst = d["pst"]
                outf = outf_fn(d["ci"], d["cols"]) if outf_fn else None
                if outf is not None:
                    P.stt(outf, pst[:], gaincol, rs[:], ALU.mult, ALU.mult)
                outb = out_fn(d["ci"], d["cols"])
                outs = outb if isinstance(outb, list) else [(slice(0, 128), outb)]
                if rope is None:
                    for (psl, o) in outs:
                        P.stt(o, pst[psl, :], gaincol[psl], rs[psl, :], ALU.mult, ALU.mult)
                    return
                xb = tmpb()
                P.stt(xb[:], pst[:], gaincol, rs[:], ALU.mult, ALU.mult)
                d.update(xb=xb, outs=outs)

            def s2(i):
                if rope is None:
                    return
                d = st[i]
                cols = d["cols"]
                pr = P.rot()
                P.mm(pr[:], permb, d["xb"][:])
                t1 = tmpf()
                P.tt(t1[:], d["xb"][:], rope[:, cols], ALU.mult)
                t2 = tmpf()
                P.tt(t2[:], pr[:], rope[:, 1024 + cols.start:1024 + cols.stop], ALU.mult)
                for (psl, o) in d["outs"]:
                    P.tt(o, t1[psl, :], t2[psl, :], ALU.add)

            nt = len(tiles)
            for i in range(nt + 2):
                if i < nt:
                    s0(i)
                if 0 <= i - 1 < nt:
                    s1(i - 1)
                if 0 <= i - 2 < nt:
                    s2(i - 2)

        def knorm_tm(pst, ng, gainrow, outf):
            nc_ = ng * 64
            sq = tmpf()
            P.act(sq[:, :nc_], pst[:, :nc_], AF.Square)
            ss = small()
            P.op("dve", lambda e: e.tensor_reduce(out=ss[:, :ng], in_=sq[:, :nc_].rearrange("p (g d) -> p g d", d=64),
                                                  axis=AX.X, op=ALU.add), reads=[sq[:, :nc_]], writes=[ss[:, :ng]])
            s2 = small()
            P.act(s2[:, :ng], ss[:, :ng], AF.Sqrt, bias=epsc, scale=1.0 / 64.0)
            P.recip(s2[:, :ng], s2[:, :ng])
            for gi in range(ng):
                P.stt(outf[:, gi * 64:(gi + 1) * 64], pst[:, gi * 64:(gi + 1) * 64], s2[:, gi:gi + 1], gainrow, ALU.mult, ALU.mult)

        def sin_layer(zout, w, kin, rhs, L, bcol):
            for c0 in range(0, L, 512):
                n = min(512, L - c0)
                pst = P.rot()
                P.mm(pst[:64, :n], w, rhs[:kin, c0:c0 + n])
                tu = tmpf()
                P.ts(tu[:64, :n], pst[:64, :n], bcol, ALU.add, 1.0 / (2.0 * math.pi), ALU.mult)
                ti = itmp
                P.ts(ti[:64, :n], tu[:64, :n], 8.5, ALU.add)
                tf_ = tmpf()
                P.copy(tf_[:64, :n], ti[:64, :n])
                P.stt(tu[:64, :n], tu[:64, :n], 8.5, tf_[:64, :n], ALU.add, ALU.subtract)
                P.ts(tf_[:64, :n], tu[:64, :n], 0.0, ALU.is_lt)
                P.tt(tu[:64, :n], tu[:64, :n], tf_[:64, :n], ALU.add)
                P.act(zout[:, c0:c0 + n], tu[:64, :n], AF.Sin, bias=negpi[:64], scale=2.0 * math.pi)

        itmp = alloc(top, [128, 512], I32, "itmp")

        def fwd_views(L):
            ntk = L // 128
            if L == 1024:
                sa = wslot()
                fa = wview(sa, 8, 1024)
                P.dma(fa, wsrc(fwd_d[L][:, 0:1024]), queue="pool")
                sb_ = wslot()
                fb = wview(sb_, 8, 1024)
                P.dma(fb, wsrc(fwd_d[L][:, 1024:2048]), queue="pool")
                return (lambda tk, i: fa[:, tk, i * 128:(i + 1) * 128]), (lambda tk, i: fb[:, tk, i * 128:(i + 1) * 128])
            s = wslot()
            f = wview(s, 2, 512)
            P.dma(f, wsrc(fwd_d[L]), queue="pool")
            return (lambda tk, i: f[:, tk, i * 128:(i + 1) * 128]), (lambda tk, i: f[:, tk, 256 + i * 128:256 + (i + 1) * 128])

        def filtergen(l, g, Ksp):
            L = 256 if g == 0 else 1024
            ntk = L // 128
            fo = 0 if g == 0 else 256
            with ExitStack() as ph:
                w1 = alloc(ph, [128, 64], F32, "hw1")
                w2 = alloc(ph, [128, 64], F32, "hw2")
                w3 = alloc(ph, [128, 1024], F32, "hw3")
                z1 = alloc(ph, [64, L], F32, "z1")
                z2 = alloc(ph, [64, L], F32, "z2")
                hp = alloc(ph, [128, ntk, 512], BF16, "hp")
                hm = alloc(ph, [128, ntk, 512], BF16, "hm")
                wst = Ring([alloc(ph, [128, 512], F32, "wst") for _ in range(2)])
                rn = alloc(ph, [128, 512], F32, "rn")
                P.dma(w1[:33, :], hy_w1[l])
                P.dma(w2[:64, :], hy_w2[l])
                P.dma(w3[:64, :], hy_w3[l])
                sin_layer(z1, w1[:33, :], 33, feats[:, fo:fo + L], L, ppv(l, "hb1")[:64])
                sin_layer(z2, w2[:64, :], 64, z1, L, ppv(l, "hb2")[:64])
                pn = FIX[0]
                for tk in range(ntk):
                    ws = wst()
                    P.dma(ws[:], win_d[L][tk * 128:(tk + 1) * 128, :])
                    hfb = []
                    for fb_ in range(2):
                        pst = P.rot()
                        P.mm(pst[:], z2[:, tk * 128:(tk + 1) * 128], w3[:64, fb_ * 512:(fb_ + 1) * 512])
                        h_ = tmpf()
                        P.tt(h_[:], pst[:], ws[:], ALU.mult)
                        ab = tmpf()
                        P.act(ab[:], h_[:], AF.Abs)
                        P.mm(pn[:], onesf[:], ab[:], start=(tk == 0 and fb_ == 0), stop=(tk == ntk - 1 and fb_ == 1))
                        hfb.append(h_)
                    if tk == 0:
                        P.memset(hfb[1][0:1, :], 0.0)
                    P.tt(hp[:, tk, :], hfb[0][:], hfb[1][:], ALU.add)
                    P.tt(hm[:, tk, :], hfb[0][:], hfb[1][:], ALU.subtract)
                P.recip(rn[:], pn[:])
                cosw, sinw = fwd_views(L)
                for i in range(ntk):
                    for cs, (wf, hsrc) in enumerate(((cosw, hp), (sinw, hm))):
                        pst = P.rot()
                        for tk in range(ntk):
                            P.mm(pst[:], wf(tk, i), hsrc[:, tk, :], start=(tk == 0), stop=(tk == ntk - 1))
                        P.tt(Ksp[:, cs, i, :], pst[:], rn[:], ALU.mult)
                P.barrier()

        def hyena(l, g, yb0, Ksp):
            L = 256 if g == 0 else 1024
            ntk = L // 128
            nseq = T // L
            W = w_in[l]
            cw = ppv(l, "cw").rearrange("p (c j) -> p c j", j=3)
            cbv = ppv(l, "cb")
            skip = ppv(l, "skip")
            with ExitStack() as ph:
                uT = alloc(ph, [128, 4, T], F32, "uT")
                with ExitStack() as ph2:
                    hst = Ring([alloc(ph2, [128, T], F32, "hst") for _ in range(2)])
                    ctmp = alloc(ph2, [128, T], F32, "ctmp")
                    for blk in range(3):
                        s = wslot()
                        wv = wview(s, 8, 512)
                        P.dma(wv, wsrc(W[:, blk * 512:(blk + 1) * 512]), queue="pool")
                        for c4 in range(4):
                            ci = blk * 4 + c4
                            hs = hst()
                            for half in range(2):
                                cols = slice(half * 512, half * 512 + 512)
                                pst = P.rot()
                                for k in range(KD):
                                    P.mm(pst[:], wv[:, k, c4 * 128:(c4 + 1) * 128], hT[:, k, cols], start=(k == 0), stop=(k == KD - 1))
                                P.copy(hs[:, cols], pst[:], eng="act")
                            cvo = uT[:, c4, :] if blk == 0 else ctmp[:]
                            P.ts(cvo, hs[:], cw[:, ci, 1:2], ALU.mult, cbv[:, ci:ci + 1], ALU.add)
                            c3 = cvo.rearrange("p (s t) -> p s t", s=nseq)
                            h3 = hs[:].rearrange("p (s t) -> p s t", s=nseq)
                            P.stt(c3[:, :, 1:L], h3[:, :, 0:L - 1], cw[:, ci, 0:1], c3[:, :, 1:L], ALU.mult, ALU.add)
                            P.stt(c3[:, :, 0:L - 1], h3[:, :, 1:L], cw[:, ci, 2:3], c3[:, :, 0:L - 1], ALU.mult, ALU.add)
                            if blk == 1:
                                P.copy(yb0[:, c4, :], ctmp[:], eng="act")
                            elif blk == 2:
                                P.tt(uT[:, c4, :], uT[:, c4, :], ctmp[:], ALU.mult)
                    P.barrier()
                u_tm = alloc(ph, [128, 8, 512], BF16, "u_tm")
                Y = alloc(ph, [128, 16, 512], BF16, "Yspec")
                for tk in range(8):
                    pst = P.rot()
                    for c4 in range(4):
                        P.tr(pst[:, c4 * 128:(c4 + 1) * 128], uT[:, c4, tk * 128:(tk + 1) * 128], identf)
                    P.copy(u_tm[:, tk, :], pst[:], eng="act")
                cosw, sinw = fwd_views(L)
                for sq_ in range(nseq):
                    for i in range(ntk):
                        pc = P.rot()
                        psn = P.rot()
                        for tk in range(ntk):
                            P.mm(pc[:], cosw(tk, i), u_tm[:, sq_ * ntk + tk, :], start=(tk == 0), stop=(tk == ntk - 1))
                        for tk in range(ntk):
                            P.mm(psn[:], sinw(tk, i), u_tm[:, sq_ * ntk + tk, :], start=(tk == 0), stop=(tk == ntk - 1))
                        Kc = Ksp[:, 0, i, :]
                        Ks = Ksp[:, 1, i, :]
                        t1 = tmpf()
                        t2 = tmpf()
                        P.tt(t1[:], pc[:], Kc, ALU.mult)
                        P.tt(t2[:], psn[:], Ks, ALU.mult)
                        P.tt(Y[:, sq_ * 2 * ntk + i, :], t1[:], t2[:], ALU.subtract)
                        t3 = tmpf()
                        t4 = tmpf()
                        P.tt(t3[:], psn[:], Kc, ALU.mult)
                        P.tt(t4[:], pc[:], Ks, ALU.mult)
                        P.tt(Y[:, sq_ * 2 * ntk + ntk + i, :], t3[:], t4[:], ALU.add)

                def epilogue(c4, cols, pv, n):
                    t = tmpf()
                    P.stt(t[:, :n], uT[:, c4, cols], skip[:, c4:c4 + 1], pv, ALU.mult, ALU.add)
                    P.tt(yb0[:, c4, cols], t[:, :n], yb0[:, c4, cols], ALU.mult)

                if L == 1024:
                    for th in range(2):
                        s = wslot()
                        iv = wview(s, 16, 512)
                        P.dma(iv, wsrc(inv_d[L][:, th * 512:(th + 1) * 512]), queue="pool")
                        for c4 in range(4):
                            pst = P.rot()
                            for j in range(16):
                                P.mm(pst[:], Y[:, j, c4 * 128:(c4 + 1) * 128], iv[:, j, :], start=(j == 0), stop=(j == 15))
                            epilogue(c4, slice(th * 512, th * 512 + 512), pst[:], 512)
                else:
                    s = wslot()
                    iv = wview(s, 4, 256)
                    P.dma(iv, wsrc(inv_d[L]), queue="pool")
                    for sq_ in range(nseq):
                        for c4 in range(4):
                            pst = P.rot()
                            for j in range(4):
                                P.mm(pst[:, :256], Y[:, sq_ * 4 + j, c4 * 128:(c4 + 1) * 128], iv[:, j, :], start=(j == 0), stop=(j == 3))
                            epilogue(c4, slice(sq_ * 256, sq_ * 256 + 256), pst[:, :256], 256)
                P.barrier()

        def layer_tables(l):
            dl = ppv(l, "dlam")
            a = small()
            pr = tmpf()
            P.tt(pr[:, 0:64], dl[:, 0:64], dl[:, 64:128], ALU.mult)
            P.tt(pr[:, 64:128], dl[:, 128:192], dl[:, 192:256], ALU.mult)
            P.op("dve", lambda e: e.tensor_reduce(out=a[:, 0:2], in_=pr[:, 0:128].rearrange("p (g d) -> p g d", d=64), axis=AX.X, op=ALU.add),
                 reads=[pr[:, 0:128]], writes=[a[:, 0:2]])
            P.act(a[:, 2:4], a[:, 0:2], AF.Exp)
            lam_init = 0.8 - 0.6 * math.exp(-0.3 * l)
            P.stt(lt[:, 0:1], a[:, 3:4], -lam_init, a[:, 2:3], ALU.add, ALU.subtract)
            P.ts(lt[:, 38:39], ppv(l, "dsub"), 1.0 - lam_init, ALU.mult)
            P.act(lt[:, 2:10], ppv(l, "sink"), AF.Exp)
            for (src, dst) in ((ppv(l, "rdf"), lt[:, 10:14]), (ppv(l, "rdb"), lt[:, 14:18])):
                e_ = small()
                P.act(e_[:, 0:4], src, AF.Exp, scale=-1.0)
                P.ts(e_[:, 0:4], e_[:, 0:4], 1.0, ALU.add)
                P.act(e_[:, 4:8], e_[:, 0:4], AF.Ln)
                P.ts(dst, e_[:, 4:8], -1.0, ALU.mult)
            P.ts(lt[:, 34:38], lt[:, 14:18], -1.0, ALU.mult)
            for c in range(2):
                for hh in range(2):
                    h = 2 * c + hh
                    ps_ = slice(64 * hh, 64 * hh + 64)
                    P.copy(lt[ps_, 18 + c:19 + c], lt[ps_, 10 + h:11 + h])
                    P.copy(lt[ps_, 20 + c:21 + c], lt[ps_, 14 + h:15 + h])
            for h in range(4):
                P.act(lt[:, 22 + h:23 + h], ccv("cola"), AF.Exp, scale=lt[:, 10 + h:11 + h])
                P.act(lt[:, 26 + h:27 + h], ccv("colb"), AF.Exp, scale=lt[:, 14 + h:15 + h])
                d1 = tmpf()
                d2 = tmpf()
                P.act(d1[:, :128], ccv("rel"), AF.Exp, scale=lt[:, 10 + h:11 + h])
                P.tt(d1[:, :128], d1[:, :128], ccv("mle"), ALU.mult)
                P.act(d2[:, :128], ccv("rel"), AF.Exp, scale=lt[:, 34 + h:35 + h])
                P.tt(d2[:, :128], d2[:, :128], ccv("mge"), ALU.mult)
                P.tt(DM[:, h, :], d1[:, :128], d2[:, :128], ALU.add)
            for c in range(2):
                P.act(QD[:, 0, c, :], ccv("iota1"), AF.Exp, scale=lt[:, 18 + c:19 + c])
                P.act(QD[:, 1, c, :], ccv("iotar"), AF.Exp, scale=lt[:, 20 + c:21 + c])
                P.act(lt[:, 30 + c:31 + c], lt[:, 18 + c:19 + c], AF.Exp, scale=128.0)
                P.act(lt[:, 32 + c:33 + c], lt[:, 20 + c:21 + c], AF.Exp, scale=128.0)

        def diff_attn(l, g, yb1):
            W = w_in[l]
            nk = T + (512 if g == 1 else 0)
            nvc = 8 + (4 if g == 1 else 0)
            rp = (lambda cols: cols) if g == 1 else (lambda cols: None)
            with ExitStack() as ph:
                rope = None
                if g == 1:
                    rope = alloc(ph, [128, 2048], F32, "rope")
                    P.dma(rope[:], rope_d)
                qm = alloc(ph, [128, 2, 4, T], BF16, "dqm")
                P.memset(qm[0:64, 1, :, :], 0.0, eng="pool")
                P.memset(qm[64:128, 0, :, :], 0.0, eng="pool")
                kT = alloc(ph, [128, 4, nk], BF16, "dkT")
                v_tm = alloc(ph, [128, nvc, 512], BF16, "dv")
                s = wslot()
                wv = wview(s, 8, 512)
                P.dma(wv, wsrc(W[:, 1536:2048]), queue="pool")
                proj_qk_pipe(wv, 4, ppv(l, "dqn"),
                             lambda ci, cols: [(slice(0, 64), qm[0:64, 0, ci, cols]), (slice(64, 128), qm[64:128, 1, ci, cols])], rope)
                s = wslot()
                wv = wview(s, 8, 512)
                P.dma(wv, wsrc(W[:, 2048:2560]), queue="pool")
                knf = alloc(ph, [128, 4, T], F32, "knf") if g == 0 else None
                proj_qk_pipe(wv, 4, ppv(l, "dkn"), lambda ci, cols: kT[:, ci, cols], rope,
                             outf_fn=(lambda ci, cols: knf[:, ci, cols]) if g == 0 else None)
                import os
                if g == 0 and not os.environ.get("NODIFFK"):
                    for tk in range(8):
                        pst = P.rot()
                        for ci in range(4):
                            P.tr(pst[:, ci * 128:(ci + 1) * 128], knf[:, ci, tk * 128:(tk + 1) * 128], identf)
                        ost = ostage()
                        P.copy(ost[:, 0:512], pst[:], eng="act")
                        P.dma(ndk[tk // 2, l, (tk % 2) * 128:(tk % 2) * 128 + 128, :], ost[:, 0:512])
                s = wslot()
                wv = wview(s, 8, 512)
                P.dma(wv, wsrc(W[:, 2560:3072]), queue="pool")

                def vcb(tk, pst):
                    P.copy(v_tm[:, tk, :], pst[:], eng="act")
                    if g == 0:
                        ost = ostage()
                        P.copy(ost[:, 0:512], pst[:])
                        P.dma(ndv[tk // 2, l, (tk % 2) * 128:(tk % 2) * 128 + 128, :], ost[:, 0:512])
                proj_tm(wv, 512, vcb)
                if g == 1:
                    for tk in range(4):
                        st = ostage()
                        P.dma(st[:, 0:512], cdk[l, tk * 128:(tk + 1) * 128, :])
                        pst = P.rot()
                        for h in range(4):
                            P.tr(pst[:, h * 128:(h + 1) * 128], st[:, h * 128:(h + 1) * 128], identf)
                        P.copy(kT[:, :, T + tk * 128:T + (tk + 1) * 128], pst[:].rearrange("p (h t) -> p h t", h=4))
                    P.dma(v_tm[:, 8:12, :], wsrc(cdv[l]), queue="pool")
                nlam = lt[:, 0:1]
                gsub = lt[:, 38:39]
                osb = [alloc(ph, [128, 512], F32, "osb") for _ in range(2)]
                if g == 0:
                    jobs = [(sq_ * 256, 256, [(sq_ * 256 + j * 128, sq_ * 2 + j) for j in range(2)]) for sq_ in range(4)]
                else:
                    kl = [(j * 128, j) for j in range(8)] + [(T + j * 128, 8 + j) for j in range(4)]
                    jobs = [(0, 512, kl), (512, 512, kl)]
                items = []
                for (q0, n, kl) in jobs:
                    for h in range(4):
                        for c in range(2):
                            for ji, (k0, vch) in enumerate(kl):
                                st = {}

                                def fS(st=st, h=h, c=c, k0=k0, q0=q0, n=n):
                                    st["pS"] = P.rot()
                                    P.mm(st["pS"][:, :n], kT[:, h, k0:k0 + 128], qm[:, c, h, q0:q0 + n])

                                def fE(st=st, n=n):
                                    st["pT"] = tmpb()
                                    P.act(st["pT"][:, :n], st["pS"][:, :n], AF.Exp, scale=0.125)

                                def fP(st=st, h=h, c=c, vch=vch, n=n, first=(ji == 0), last=(ji == len(kl) - 1)):
                                    P.mm(FIX[2 * c][:, :n], v_tm[:, vch, h * 128:(h + 1) * 128], st["pT"][:, :n], start=first, stop=last)
                                    P.mm(FIX[2 * c + 1][:, :n], onesb, st["pT"][:, :n], start=first, stop=last)

                                fEnd = None
                                if ji == len(kl) - 1:
                                    def fEnd(h=h, c=c, q0=q0, n=n):
                                        r = tmpf()
                                        P.act(r[:, :n], FIX[2 * c + 1][:, :n], AF.Ln)
                                        P.act(r[:, :n], r[:, :n], AF.Exp, scale=-1.0)
                                        P.tt(osb[c][:, :n], FIX[2 * c][:, :n], r[:, :n], ALU.mult)
                                        if c == 1:
                                            y = tmpf()
                                            P.stt(y[:, :n], osb[1][:, :n], nlam, osb[0][:, :n], ALU.mult, ALU.add)
                                            sq = tmpb()
                                            P.act(sq[:, :n], y[:, :n], AF.Square)
                                            pn = P.rot()
                                            P.mm(pn[:, :n], ones128b, sq[:, :n])
                                            rs = rstd_from(pn[:, :n], n)
                                            P.stt(yb1[:, h, q0:q0 + n], y[:, :n], gsub, rs[:, :n], ALU.mult, ALU.mult)
                                items.append((fS, fE, fP, fEnd))
                run_pipe(items)
                P.barrier()

        def win_attn(l, g, yb2):
            W = w_in[l]
            nk = T + (512 if g == 1 else 0)
            nvc = 8 + (4 if g == 1 else 0)
            rp = (lambda cols: cols) if g == 1 else (lambda cols: None)
            with ExitStack() as ph:
                rope = None
                if g == 1:
                    rope = alloc(ph, [128, 2048], F32, "rope")
                    P.dma(rope[:], rope_d)
                qwm = alloc(ph, [128, 2, 4, T], BF16, "wqm")
                P.memset(qwm[0:64, 1, :, :], 0.0, eng="pool")
                P.memset(qwm[64:128, 0, :, :], 0.0, eng="pool")
                kw2 = alloc(ph, [128, 2, nk], BF16, "wk2")
                vw = alloc(ph, [128, nvc, 256], BF16, "wv")
                s = wslot()
                wv = wview(s, 8, 512)
                P.dma(wv, wsrc(W[:, 3072:3584]), queue="pool")
                proj_qk_pipe(wv, 4, ppv(l, "wqn"),
                             lambda ci, cols: [(slice(0, 64), qwm[0:64, 0, ci, cols]), (slice(64, 128), qwm[64:128, 1, ci, cols])], rope)
                s = wslot()
                wk = wview(s, 8, 512)
                for hk in range(2):
                    for dup in range(2):
                        P.dma(wk[:, :, (hk * 2 + dup) * 64:(hk * 2 + dup + 1) * 64], wsrc(W[:, 3584 + hk * 64:3584 + (hk + 1) * 64]), queue="pool")
                P.dma(wk[:, :, 256:384], wsrc(W[:, 3584:3712]), queue="pool")
                P.dma(wk[:, :, 384:512], wsrc(W[:, 3712:3840]), queue="pool")
                knf = alloc(ph, [128, 2, T], F32, "wknf") if g == 0 else None
                proj_qk_pipe(wk, 2, ppv(l, "wkn"), lambda ci, cols: kw2[:, ci, cols], rope,
                             outf_fn=(lambda ci, cols: knf[:, ci, cols]) if g == 0 else None)
                import os
                if g == 0 and not os.environ.get("NOWINK"):
                    for tk in range(8):
                        pst = P.rot()
                        for ci in range(2):
                            P.tr(pst[:, ci * 128:(ci + 1) * 128], knf[:, ci, tk * 128:(tk + 1) * 128], identf)
                        ost = ostage()
                        P.copy(ost[:, 0:256], pst[:, 0:256], eng="act")
                        r0 = (tk % 2) * 128
                        P.dma(nwk[tk // 2, l, r0:r0 + 128, 0:64], ost[:, 0:64])
                        P.dma(nwk[tk // 2, l, r0:r0 + 128, 64:128], ost[:, 192:256])

                def vcb(tk, pst):
                    v4 = vw[:, tk, :].rearrange("p (h u d) -> p h u d", h=2, u=2)
                    p3 = pst[:, 0:128].rearrange("p (h d) -> p h d", h=2)
                    P.copy(v4[:, :, 0, :], p3, eng="act")
                    P.copy(v4[:, :, 1, :], p3, eng="act")
                    if g == 0:
                        ost = ostage()
                        P.copy(ost[:, 0:128], pst[:, 0:128])
                        P.dma(nwv[tk // 2, l, (tk % 2) * 128:(tk % 2) * 128 + 128, :], ost[:, 0:128])
                proj_tm(wk, 128, vcb, c0=384)
                if g == 1:
                    st = ostage()
                    st5 = st[:].rearrange("p (k h u d) -> p k h u d", k=4, h=2, u=2)
                    src = cwk[l].rearrange("(k p) (h d) -> p k h d", p=128, h=2)
                    for dup in range(2):
                        for tk in range(4):
                            P.dma(st5[:, tk, :, dup, :], src[:, tk])
                    st3 = st[:].rearrange("p (k n) -> p k n", k=4)
                    for tk in range(4):
                        pst = P.rot()
                        for hk in range(2):
                            P.tr(pst[:, hk * 128:(hk + 1) * 128], st3[:, tk, hk * 128:(hk + 1) * 128], identf)
                        P.copy(kw2[:, :, T + tk * 128:T + (tk + 1) * 128], pst[:, 0:256].rearrange("p (h t) -> p h t", h=2))
                    vsrc = cwv[l].rearrange("(k p) (h d) -> p k h d", p=128, h=2)
                    for tk in range(4):
                        v4 = vw[:, 8 + tk, :].rearrange("p (h u d) -> p h u d", h=2, u=2)
                        for dup in range(2):
                            P.dma(v4[:, :, dup, :], vsrc[:, tk], queue="pool")
                esink = lt[:, 2:10]
                items = []
                gcount = [0]

                def mk_finish(h, q0, n, pO, pZ):
                    def fin():
                        po = slice(64 * (h % 2), 64 * (h % 2) + 64)
                        r = tmpf()
                        P.act(r[po, :n], pZ[po, :n], AF.Ln, bias=esink[po, h:h + 1])
                        P.act(r[po, :n], r[po, :n], AF.Exp, scale=-1.0)
                        P.tt(yb2[po, h // 2, q0:q0 + n], pO[po, :n], r[po, :n], ALU.mult)
                    return fin

                def add_item(h, kcols, qcols, vch, n, cs, first, last, mask, pO, pZ, fEnd):
                    hk = h // 4
                    c4 = h // 2
                    vc = slice(hk * 128, hk * 128 + 128)
                    st = {}

                    def fS():
                        st["pS"] = P.rot()
                        P.mm(st["pS"][:, :n], kw2[:, hk, kcols], qwm[:, h % 2, c4, qcols])

                    def fE():
                        st["pT"] = tmpb()
                        P.act(st["pT"][:, :n], st["pS"][:, :n], AF.Exp, scale=0.125)
                        if mask is not None:
                            P.tt(st["pT"][:, :n], st["pT"][:, :n], mask, ALU.mult)

                    def fP():
                        P.mm(pO[:, cs], vw[:, vch, vc], st["pT"][:, :n], start=first, stop=last)
                        P.mm(pZ[:, cs], onesb, st["pT"][:, :n], start=first, stop=last)
                    items.append((fS, fE, fP, fEnd))

                for h in range(8):
                    if g == 0:
                        for sq_ in range(4):
                            q0 = sq_ * 256
                            pO, pZ = FIX[2 * (gcount[0] % 2)], FIX[2 * (gcount[0] % 2) + 1]
                            gcount[0] += 1
                            for j in range(2):
                                add_item(h, slice(q0 + j * 128, q0 + (j + 1) * 128), slice(q0, q0 + 256), sq_ * 2 + j, 256, slice(0, 256),
                                         j == 0, j == 1, None, pO, pZ, mk_finish(h, q0, 256, pO, pZ) if j == 1 else None)
                    else:
                        for half in range(2):
                            q0 = half * 512
                            pO, pZ = FIX[2 * (gcount[0] % 2)], FIX[2 * (gcount[0] % 2) + 1]
                            gcount[0] += 1
                            for j in range(4):
                                add_item(h, slice(T + j * 128, T + (j + 1) * 128), slice(q0, q0 + 512), 8 + j, 512, slice(0, 512),
                                         j == 0, False, None, pO, pZ, None)
                            for qb in range(4):
                                nb = half * 4 + qb
                                qc0 = nb * 128
                                cs = slice(qb * 128, qb * 128 + 128)
                                kbs = [kb for kb in (nb - 1, nb, nb + 1) if 0 <= kb <= 7]
                                for kb in kbs:
                                    last = (qb == 3 and kb == kbs[-1])
                                    mask = mge_b if kb == nb - 1 else (mle_b if kb == nb + 1 else None)
                                    add_item(h, slice(kb * 128, (kb + 1) * 128), slice(qc0, qc0 + 128), kb, 128, cs,
                                             False, last, mask, pO, pZ, mk_finish(h, q0, 512, pO, pZ) if last else None)
                run_pipe(items)
                P.barrier()

        def retention(l, g, yb3):
            W = w_in[l]
            L = 256 if g == 0 else 1024
            ntk = L // 128
            nseq = T // L
            with ExitStack() as ph:
                rq = alloc(ph, [128, 2, T], BF16, "rq")
                rk = alloc(ph, [128, 2, T], BF16, "rk")
                rk_tm = alloc(ph, [128, 8, 256], BF16, "rk_tm")
                rv = alloc(ph, [128, 8, 512], BF16, "rv")
                rg = alloc(ph, [128, 4, T], BF16, "rg")
                Qd = alloc(ph, [128, 2, 2, T], BF16, "Qd")
                Kd = alloc(ph, [128, 2, 8, 256], BF16, "Kd")
                SB = alloc(ph, [128, 2, 2, 8, 128], BF16, "SB")
                Srun = alloc(ph, [128, 2, 2, 128], F32, "Srun")
                s = wslot()
                wv = wview(s, 8, 512)
                P.dma(wv, wsrc(W[:, 3840:4352]), queue="pool")

                def qkcb(ci, half, cols, pst):
                    if ci < 2:
                        P.copy(rq[:, ci, cols], pst[:], eng="act")
                    else:
                        P.act(rk[:, ci - 2, cols], pst[:], AF.Copy, scale=0.125)
                proj_fm(wv, 4, qkcb)
                proj_tm(wv, 256, lambda tk, pst: P.act(rk_tm[:, tk, :], pst[:, 0:256], AF.Copy, scale=0.125), c0=256)
                s = wslot()
                wv = wview(s, 8, 512)
                P.dma(wv, wsrc(W[:, 4352:4864]), queue="pool")
                proj_tm(wv, 512, lambda tk, pst: P.copy(rv[:, tk, :], pst[:], eng="act"))
                s = wslot()
                wv = wview(s, 8, 512)
                P.dma(wv, wsrc(W[:, 4864:5376]), queue="pool")
                proj_fm(wv, 4, lambda ci, half, cols, pst: P.act(rg[:, ci, cols], pst[:], AF.Silu))
                for d_ in range(2):
                    for c in range(2):
                        for k8 in range(8):
                            P.tt(Qd[:, d_, c, k8 * 128:(k8 + 1) * 128], rq[:, c, k8 * 128:(k8 + 1) * 128], QD[:, d_, c, :], ALU.mult)
                    for h in range(4):
                        P.ts(Kd[:, d_, :, 64 * h:64 * h + 64], rk_tm[:, :, 64 * h:64 * h + 64], lt[:, 22 + 4 * d_ + h:23 + 4 * d_ + h], ALU.mult)
                for sq_ in range(nseq):
                    tb = sq_ * ntk
                    if g == 1:
                        for d_, src in ((0, srf), (1, srb)):
                            P.dma(Srun[:, d_, :, :], src[l].rearrange("(c hh) d e -> (hh d) c e", hh=2))
                    else:
                        P.memset(Srun[:], 0.0)
                    for d_ in range(2):
                        order = list(range(ntk)) if d_ == 0 else list(range(ntk - 1, -1, -1))
                        for n_ in order:
                            gn = tb + n_
                            for c in range(2):
                                P.copy(SB[:, d_, c, gn, :], Srun[:, d_, c, :], eng="pool")
                                pD = P.rot()
                                for hh in range(2):
                                    h = 2 * c + hh
                                    P.mm(pD[64 * hh:64 * hh + 64, 0:128], Kd[:, d_, gn, 64 * h:64 * h + 64], rv[:, gn, 128 * h:128 * h + 128])
                                P.stt(Srun[:, d_, c, :], Srun[:, d_, c, :], lt[:, 30 + 2 * d_ + c:31 + 2 * d_ + c], pD[:, 0:128], ALU.mult, ALU.add)
                    if g == 0:
                        for d_, dst in ((0, nrf), (1, nrb)):
                            P.dma(dst[sq_, l].rearrange("(c hh) d e -> (hh d) c e", hh=2), Srun[:, d_, :, :])
                atr = Ring([alloc(ph, [128, 128], BF16, "atr") for _ in range(8)])
                items = []
                for h in range(4):
                    for half in range(2):
                        for qb in range(4):
                            st = {}

                            def fS(st=st, h=h, half=half, qb=qb):
                                c = h // 2
                                ps_ = slice(64 * (h % 2), 64 * (h % 2) + 64)
                                gn = half * 4 + qb
                                tcs = slice(gn * 128, gn * 128 + 128)
                                st["pS"] = P.rot()
                                P.mm(st["pS"][:, :128], rk[ps_, c, tcs], rq[ps_, c, tcs])

                            def fE(st=st, h=h):
                                st["At"] = atr()
                                P.tt(st["At"][:, :128], st["pS"][:, :128], DM[:, h, :], ALU.mult)

                            def fP(st=st, h=h, half=half, qb=qb):
                                c = h // 2
                                ps_ = slice(64 * (h % 2), 64 * (h % 2) + 64)
                                gn = half * 4 + qb
                                cs = slice(qb * 128, qb * 128 + 128)
                                tcs = slice(gn * 128, gn * 128 + 128)
                                pOut = FIX[(h * 2 + half) % 4]
                                P.mm(pOut[:, cs], rv[:, gn, 128 * h:128 * h + 128], st["At"][:, :128], start=True, stop=False)
                                P.mm(pOut[:, cs], SB[ps_, 0, c, gn, :], Qd[ps_, 0, c, tcs], start=False, stop=False)
                                P.mm(pOut[:, cs], SB[ps_, 1, c, gn, :], Qd[ps_, 1, c, tcs], start=False, stop=True)

                            fEnd = None
                            if qb == 3:
                                def fEnd(h=h, half=half):
                                    pOut = FIX[(h * 2 + half) % 4]
                                    cols = slice(half * 512, half * 512 + 512)
                                    y = tmpf()
                                    P.copy(y[:], pOut[:], eng="act")
                                    sq = tmpb()
                                    P.act(sq[:], y[:], AF.Square)
                                    pn = P.rot()
                                    P.mm(pn[:], ones128b, sq[:])
                                    rs = rstd_from(pn[:], 512)
                                    t = tmpf()
                                    P.tt(t[:], y[:], rs[:], ALU.mult)
                                    P.tt(yb3[:, h, cols], t[:], rg[:, h, cols], ALU.mult)
                            items.append((fS, fE, fP, fEnd))
                run_pipe(items)
                P.barrier()

        def merge(l, g, yb):
            W = w_in[l]
            with ExitStack() as ph:
                mg = alloc(ph, [128, KD, T], BF16, "merged")
                mtmp = Ring([alloc(ph, [128, 512], F32, "mtmp") for _ in range(12)])
                for m in range(KD):
                    sa = wslot()
                    wb = sa[:, 0:2048].rearrange("p (b k n) -> p b k n", b=4, k=4)
                    wg = sa[:, 2048:6144].rearrange("p (b k n) -> p b k n", b=4, k=8)
                    for b in range(4):
                        P.dma(wb[:, b], wsrc(w_branch[l, b][:, m * 128:(m + 1) * 128]), queue="pool")
                    for b in range(4):
                        P.dma(wg[:, b], wsrc(W[:, 5376 + b * 1024 + m * 128:5376 + b * 1024 + (m + 1) * 128]), queue="pool")
                    for half in range(2):
                        cols = slice(half * 512, half * 512 + 512)
                        acc = None
                        for b in range(4):
                            pB = P.rot()
                            for kc in range(4):
                                P.mm(pB[:], wb[:, b, kc, :], yb[b][:, kc, cols], start=(kc == 0), stop=(kc == 3))
                            pG = P.rot()
                            for k in range(KD):
                                P.mm(pG[:], wg[:, b, k, :], hT[:, k, cols], start=(k == 0), stop=(k == KD - 1))
                            sg = mtmp()
                            P.act(sg[:], pG[:], AF.Sigmoid)
                            t = mtmp()
                            P.tt(t[:], pB[:], sg[:], ALU.mult)
                            if b == 0:
                                acc = t
                            elif b < 3:
                                t2 = mtmp()
                                P.tt(t2[:], t[:], acc[:], ALU.add)
                                acc = t2
                            else:
                                P.tt(mg[:, m, cols], t[:], acc[:], ALU.add)
                s = wslot()
                wo = wview(s, 8, 1024)
                P.dma(wo, wsrc(w_out[l]), queue="pool")
                for m in range(KD):
                    for half in range(2):
                        cols = slice(half * 512, half * 512 + 512)
                        pst = P.rot()
                        for k in range(KD):
                            P.mm(pst[:], wo[:, k, m * 128:(m + 1) * 128], mg[:, k, cols], start=(k == 0), stop=(k == KD - 1))
                        P.stt(xT[:, m, cols], pst[:], Gtab[:, l, 1, m, g:g + 1], xT[:, m, cols], ALU.mult, ALU.add)
                P.barrier()

        def mixer(l, g):
            ada_tables(l, 1)
            rmsnorm_mod(l, 1, g)
            layer_tables(l)
            if dbg == "tables":
                st = ostage()
                P.memset(st[:], 0.0)
                P.copy(st[:, 0:64], lt[:])
                P.copy(st[:, 64:576], DM[:].rearrange("p h i -> p (h i)"))
                P.copy(st[:, 576:1088 - 64], QD[:].rearrange("p d c i -> p (d c i)")[:, 0:448])
                P.dma(dbg_d[:, 0, :], st[:])
            with ExitStack() as ms:
                yb = [None] * 4
                import os
                mc = int(os.environ.get("MIXCUT", "9"))
                yb[3] = alloc(ms, [128, 4, T], BF16, "yb3")
                if mc >= 1:
                    retention(l, g, yb[3])
                yb[0] = alloc(ms, [128, 4, T], BF16, "yb0")
                with ExitStack() as ks:
                    Ksp = alloc(ks, [128, 2, 8, 512], BF16, "Ksp")
                    if mc >= 2:
                        filtergen(l, g, Ksp)
                    if mc >= 3:
                        hyena(l, g, yb[0], Ksp)
                yb[1] = alloc(ms, [128, 4, T], BF16, "yb1")
                if mc >= 4:
                    diff_attn(l, g, yb[1])
                yb[2] = alloc(ms, [128, 4, T], BF16, "yb2")
                if mc >= 5:
                    win_attn(l, g, yb[2])
                if mc < 6:
                    return
                if dbg and dbg == (l, g):
                    for b in range(4):
                        for c4 in range(4):
                            st = ostage()
                            P.copy(st[:], yb[b][:, c4, :])
                            P.dma(dbg_d[:, b * 4 + c4, :], st[:])
                merge(l, g, yb)

        stages = []
        for g in groups:
            stages.append(("load", g, 0))
            for l in range(2):
                stages.append(("ffa", g, l))
                stages.append(("mix", g, l))
                stages.append(("ffb", g, l))
            stages.append(("store", g, 0))
        for (kind, g, l) in stages:
            if kind == "load":
                load_x(g)
            elif kind == "store":
                store_x(g)
            elif kind == "ffa":
                ffn(l, 0, g)
            elif kind == "ffb":
                ffn(l, 1, g)
            elif kind == "mix":
                mixer(l, g)
            if stop_after is not None and (kind, g, l) == tuple(stop_after):
                if kind != "store":
                    store_x(g)
                break
        P.finish()
        print("program: ninst=%d nwaits=%d" % (P.ninst, P.nwaits), "counts", P.count, "dmax", {q: max(v) for q, v in P.dcount.items()})
    return nc


_CACHE = {}


def _get_program(stop_after=None, dbg=False):
    key = (tuple(stop_after) if stop_after else None, dbg)
    if key not in _CACHE:
        _CACHE[key] = build_program(stop_after=stop_after, dbg=dbg)
    return _CACHE[key]


def make_in_maps(inp):
    f = lambda a: np.ascontiguousarray(np.asarray(a, dtype=np.float32))
    consts = _host_consts()
    pp = _host_params(inp)
    shared = {
        "w_ada": f(inp["w_ada"]), "w_ffa_in": f(inp["w_ffa_in"]), "w_ffb_in": f(inp["w_ffb_in"]),
        "w_ffa_out": f(inp["w_ffa_out"]), "w_ffb_out": f(inp["w_ffb_out"]), "w_in": f(inp["w_in"]),
        "hy_f_w1": f(inp["hy_f_w1"]), "hy_f_w2": f(inp["hy_f_w2"]), "hy_f_w3": f(inp["hy_f_w3"]),
        "w_branch": f(inp["w_branch"]), "w_out": f(inp["w_out"]), "pp": pp,
    }
    for k, v in consts.items():
        shared[k] = v
    maps = []
    for i in range(NCORES):
        m = dict(shared)
        m["x_ctx"] = f(inp["x_prompt"][4 * i:4 * i + 4]).reshape(T, D)
        m["x_lat"] = f(inp["x_sample"][i]).reshape(T, D)
        cond = np.stack([f(inp["c_ctx"]).reshape(8, 128).T, f(inp["c"][i]).reshape(8, 128).T], axis=2)
        m["cond"] = np.ascontiguousarray(cond.reshape(128, 16))
        m["cdk"] = f(inp["cache_diff_k"][i]).reshape(2, 512, 512)
        m["cdv"] = f(inp["cache_diff_v"][i]).reshape(2, 512, 512)
        m["cwk"] = f(inp["cache_win_k"][i]).reshape(2, 512, 128)
        m["cwv"] = f(inp["cache_win_v"][i]).reshape(2, 512, 128)
        m["srf"] = f(inp["state_ret_f"][i])
        m["srb"] = f(inp["state_ret_b"][i])
        maps.append(m)
    return maps


def kernel(**inputs):
    nc = _get_program()
    maps = make_in_maps(inputs)
    res = run_bass_kernel_spmd(nc, maps, core_ids=list(range(NCORES)))
    R = res.results
    y_prompt = np.concatenate([R[i]["y_ctx"].reshape(4, 256, D) for i in range(NCORES)], axis=0)
    y_sample = np.stack([R[i]["y_lat"] for i in range(NCORES)], axis=0)
    ndk = np.concatenate([R[i]["ndk"] for i in range(NCORES)], axis=0).reshape(32, 2, 256, 4, 2, 64)
    ndv = np.concatenate([R[i]["ndv"] for i in range(NCORES)], axis=0).reshape(32, 2, 256, 4, 128)
    nwk = np.concatenate([R[i]["nwk"] for i in range(NCORES)], axis=0).reshape(32, 2, 256, 2, 64)
    nwv = np.concatenate([R[i]["nwv"] for i in range(NCORES)], axis=0).reshape(32, 2, 256, 2, 64)
    nrf = np.concatenate([R[i]["nrf"] for i in range(NCORES)], axis=0)
    nrb = np.concatenate([R[i]["nrb"] for i in range(NCORES)], axis=0)
    return tuple(np.ascontiguousarray(a.astype(np.float32)) for a in (y_prompt, y_sample, ndk, ndv, nwk, nwv, nrf, nrb))
```
